# Optimizing a Trainium2 kernel written in Bass

```python
import jax, jax.numpy as jnp
from jax import lax
import numpy as np

D_MODEL = 1024
BATCH = 8
SEQ = 4096
DEPTH = 1

HEAD_DIM = 64
MIX_WIDTH = D_MODEL
ATT_WIDTH = MIX_WIDTH // 2
CONV_WIDTH = MIX_WIDTH - ATT_WIDTH
N_ATT_HEADS = ATT_WIDTH // HEAD_DIM
N_CONV_GROUPS = CONV_WIDTH // HEAD_DIM
N_GROUPS = N_ATT_HEADS + N_CONV_GROUPS
CONV_K = 3
D_FF = 256 * ((8 * D_MODEL // 3 + 255) // 256)
Q_BLOCK = 128
N_MOD = 9
EPS = 1e-6
IN_COLS = 3 * ATT_WIDTH + N_ATT_HEADS + 3 * CONV_WIDTH

kernel_name = "hybrid_fox_shortconv_macaron_adaln"


def rmsnorm(x, g):
    xf = x.astype(jnp.float32)
    y = xf * lax.rsqrt(jnp.mean(xf * xf, axis=-1, keepdims=True) + EPS)
    return (y * g.astype(jnp.float32)).astype(x.dtype)


def modulate(h, shift, scale):
    return h * (1.0 + scale[:, None, :]) + shift[:, None, :]


def swiglu(h, w_gate, w_up, w_down):
    return (jax.nn.silu(h @ w_gate) * (h @ w_up)) @ w_down


def forgetting_attention(q, k, v, log_f):
    S = q.shape[2]
    scale = 1.0 / np.sqrt(HEAD_DIM).astype(np.float32)
    F = jnp.cumsum(log_f, axis=-1)
    outs = []
    for i in range(S // Q_BLOCK):
        q0, q1 = i * Q_BLOCK, (i + 1) * Q_BLOCK
        qb = q[:, :, q0:q1]
        kb = k[:, :, :q1]
        vb = v[:, :, :q1]
        s = jnp.einsum('bhqd,bhkd->bhqk', qb, kb).astype(jnp.float32) * scale
        s = s + F[:, :, q0:q1, None] - F[:, :, None, :q1]
        qpos = q0 + jnp.arange(Q_BLOCK)
        kpos = jnp.arange(q1)
        s = jnp.where(kpos[None, :] <= qpos[:, None], s, -jnp.inf)
        p = jax.nn.softmax(s, axis=-1)
        outs.append(jnp.einsum('bhqk,bhkd->bhqd', p.astype(v.dtype), vb))
    return jnp.concatenate(outs, axis=2)


def short_conv(u, conv_w):
    S = u.shape[1]
    up = jnp.pad(u, ((0, 0), (CONV_K - 1, 0), (0, 0)))
    y = conv_w[0] * up[:, 0:S]
    for j in range(1, CONV_K):
        y = y + conv_w[j] * up[:, j:j + S]
    return y


def hybrid_mixer(h, w_in, forget_bias, conv_w, group_norm_g, w_out):
    B, S, _ = h.shape
    proj = h @ w_in
    o = 0
    q = proj[..., o:o + ATT_WIDTH]; o += ATT_WIDTH
    k = proj[..., o:o + ATT_WIDTH]; o += ATT_WIDTH
    v = proj[..., o:o + ATT_WIDTH]; o += ATT_WIDTH
    f_logit = proj[..., o:o + N_ATT_HEADS]; o += N_ATT_HEADS
    gate_b = proj[..., o:o + CONV_WIDTH]; o += CONV_WIDTH
    gate_c = proj[..., o:o + CONV_WIDTH]; o += CONV_WIDTH
    xc = proj[..., o:o + CONV_WIDTH]

    def heads(t):
        return t.reshape(B, S, N_ATT_HEADS, HEAD_DIM).transpose(0, 2, 1, 3)
    log_f = jax.nn.log_sigmoid(f_logit.astype(jnp.float32) + forget_bias.astype(jnp.float32))
    log_f = log_f.transpose(0, 2, 1)
    att = forgetting_attention(heads(q), heads(k), heads(v), log_f)
    att = att.transpose(0, 2, 1, 3).reshape(B, S, ATT_WIDTH)

    cv = gate_b * short_conv(gate_c * xc, conv_w)

    y = jnp.concatenate([att, cv], axis=-1).reshape(B, S, N_GROUPS, HEAD_DIM)
    y = rmsnorm(y, group_norm_g.reshape(N_GROUPS, HEAD_DIM))
    return y.reshape(B, S, MIX_WIDTH) @ w_out


def setup_inputs(seed: int = 0) -> dict:
    key = jax.random.key(seed)
    ks = jax.random.split(key, 20)
    f32 = jnp.float32
    D = D_MODEL

    def nrm(k, shape, fan_in):
        return jax.random.normal(k, shape, f32) * (fan_in ** -0.5)

    def gain(k, n):
        return 1.0 + 0.02 * jax.random.normal(k, (n,), f32)

    return {
        "x": jax.random.normal(ks[0], (BATCH, SEQ, D), f32),
        "c": jax.random.normal(ks[1], (BATCH, D), f32),
        "ada_w": nrm(ks[2], (D, N_MOD * D), D),
        "ada_b": 0.02 * jax.random.normal(ks[3], (N_MOD * D,), f32),
        "norm1_g": gain(ks[4], D),
        "ffn1_w_gate": nrm(ks[5], (D, D_FF), D),
        "ffn1_w_up": nrm(ks[6], (D, D_FF), D),
        "ffn1_w_down": nrm(ks[7], (D_FF, D), D_FF),
        "norm2_g": gain(ks[8], D),
        "w_in": nrm(ks[9], (D, IN_COLS), D),
        "forget_bias": 3.0 + 3.0 * jax.random.uniform(ks[10], (N_ATT_HEADS,), f32),
        "conv_w": nrm(ks[11], (CONV_K, CONV_WIDTH), CONV_K),
        "group_norm_g": gain(ks[12], MIX_WIDTH),
        "w_out": nrm(ks[13], (MIX_WIDTH, D), MIX_WIDTH),
        "norm3_g": gain(ks[14], D),
        "ffn2_w_gate": nrm(ks[15], (D, D_FF), D),
        "ffn2_w_up": nrm(ks[16], (D, D_FF), D),
        "ffn2_w_down": nrm(ks[17], (D_FF, D), D_FF),
        "final_g": gain(ks[18], D),
    }


def reference(x, c, ada_w, ada_b, norm1_g, ffn1_w_gate, ffn1_w_up, ffn1_w_down,
              norm2_g, w_in, forget_bias, conv_w, group_norm_g, w_out,
              norm3_g, ffn2_w_gate, ffn2_w_up, ffn2_w_down, final_g):
    mod = jax.nn.silu(c) @ ada_w + ada_b
    (sh1, sc1, g1, sh2, sc2, g2, sh3, sc3, g3) = jnp.split(mod, N_MOD, axis=-1)
    for _ in range(DEPTH):
        h = modulate(rmsnorm(x, norm1_g), sh1, sc1)
        x = x + 0.5 * g1[:, None, :] * swiglu(h, ffn1_w_gate, ffn1_w_up, ffn1_w_down)
        h = modulate(rmsnorm(x, norm2_g), sh2, sc2)
        x = x + g2[:, None, :] * hybrid_mixer(h, w_in, forget_bias, conv_w, group_norm_g, w_out)
        h = modulate(rmsnorm(x, norm3_g), sh3, sc3)
        x = x + 0.5 * g3[:, None, :] * swiglu(h, ffn2_w_gate, ffn2_w_up, ffn2_w_down)
    return rmsnorm(x, final_g)
```

```python
from contextlib import ExitStack
import numpy as np
import concourse.bass as bass
import concourse.mybir as mybir
from concourse.bass_utils import run_bass_kernel_spmd

F32 = mybir.dt.float32
BF16 = mybir.dt.bfloat16
AF = mybir.ActivationFunctionType
ALU = mybir.AluOpType

D = 1024
S = 4096
DFF = 2816
NFC = 22
INC = 3080
NB = 8
TB = 512
EPS = 1e-6
NEG = -30000.0
FH = ((0, 12), (12, 10))

ENG_NAMES = ("tensor", "vector", "scalar", "gpsimd", "sync")
SEM_ROLL = 30000


class Op:
    __slots__ = ("eng", "fn", "deps", "signal", "sem", "count", "is_dma")

    def __init__(self, eng, fn, deps, is_dma=False):
        self.eng = eng
        self.fn = fn
        self.deps = deps
        self.signal = False
        self.sem = None
        self.count = 0
        self.is_dma = is_dma


class Chan:
    def __init__(self, sem):
        self.sem = sem
        self.count = 0
        self.last = None


class Res:
    __slots__ = ("w", "r", "rd", "excl")

    def __init__(self, excl=False):
        self.w = None
        self.r = {}
        self.rd = []
        self.excl = excl


class Sched:
    def __init__(self, nc, stack):
        self.nc = nc
        self.stack = stack
        self.ops = {e: [] for e in ENG_NAMES}
        self.nsem = 0

    def new_sem(self, name=None):
        self.nsem += 1
        return self.stack.enter_context(self.nc.semaphore(name or f"s{self.nsem}"))

    def chan(self):
        return Chan(self.new_sem())

    def _deps(self, eng, reads, writes, extra, is_dma):
        deps = []
        wr = list(writes)
        for R in reads:
            if R.excl:
                wr.append(R)
            elif R.w is not None:
                deps.append(R.w)
        for R in wr:
            if R.w is not None:
                deps.append(R.w)
            deps.extend(R.r.values())
            deps.extend(R.rd)
        deps.extend(d for d in extra if d is not None)
        out = []
        seen = set()
        for d in deps:
            if id(d) in seen:
                continue
            seen.add(id(d))
            if (not is_dma) and (not d.is_dma) and d.eng == "tensor" and eng == "tensor":
                continue
            out.append(d)
        return out

    def _update(self, o, reads, writes):
        for R in reads:
            if R.excl:
                R.w = o
                R.r = {}
                R.rd = []
            elif o.is_dma:
                R.rd.append(o)
            else:
                R.r[o.eng] = o
        for R in writes:
            R.w = o
            R.r = {}
            R.rd = []

    def op(self, eng, fn, reads=(), writes=(), extra=()):
        deps = self._deps(eng, reads, writes, extra, False)
        o = Op(eng, fn, deps)
        for d in deps:
            d.signal = True
        self.ops[eng].append(o)
        self._update(o, reads, writes)
        return o

    def dma(self, eng, chan, out, in_, reads=(), writes=(), extra=()):
        deps = self._deps(eng, reads, writes, list(extra) + [chan.last], True)

        def fn(e):
            return e.dma_start(out=out, in_=in_)

        o = Op(eng, fn, deps, is_dma=True)
        for d in deps:
            d.signal = True
        chan.count += 16
        chan.last = o
        o.sem = chan.sem
        o.count = chan.count
        self.ops[eng].append(o)
        self._update(o, reads, writes)
        return o

    def emit(self):
        nc = self.nc
        for e in ENG_NAMES:
            sem = None
            cnt = 0
            for o in self.ops[e]:
                if o.is_dma:
                    continue
                if o.signal:
                    if o.fn is None:
                        raise RuntimeError("wait-only op cannot signal")
                    if sem is None or cnt >= SEM_ROLL:
                        sem = self.new_sem()
                        cnt = 0
                    cnt += 1
                    o.sem = sem
                    o.count = cnt
        with nc.Block() as block:
            for e in ENG_NAMES:
                ops = self.ops[e]
                if not ops:
                    continue

                def body(engh, ops=ops):
                    waited = {}
                    for o in ops:
                        for d in o.deps:
                            key = id(d.sem)
                            if waited.get(key, 0) >= d.count:
                                continue
                            engh.wait_ge(d.sem, d.count)
                            waited[key] = d.count
                        if o.fn is None:
                            continue
                        ins = o.fn(engh)
                        if o.is_dma:
                            ins.then_inc(o.sem, 16)
                        elif o.signal:
                            ins.then_inc(o.sem, 1)

                getattr(block, e)(body)


class Ring:
    def __init__(self, tiles):
        self.tiles = [(t, Res()) for t in tiles]
        self.i = 0

    def get(self):
        t = self.tiles[self.i % len(self.tiles)]
        self.i += 1
        return t


class PsumPool:
    def __init__(self, banks):
        self.free = [(b, Res(excl=True)) for b in banks]

    def alloc(self):
        if not self.free:
            raise RuntimeError("psum pool exhausted")
        return self.free.pop(0)

    def release(self, b):
        self.free.append(b)


def build_program(nblocks=NB, upto=99, fnorm=True, dbg=False):
    nc = bass.Bass("TRN2", target_bir_lowering=False)

    def din(name, shape, dt=F32):
        return nc.dram_tensor(name, shape, dt, kind="ExternalInput")

    x_d = din("x", [S, D])
    ccol_d = din("ccol", [128, 8])
    adaw_d = din("ada_w", [D, 9 * D])
    vecs_d = din("vecs", [128, 60])
    adabc_d = din("adabc", [128, 72])
    w1g_d = din("w1g", [D, DFF])
    w1u_d = din("w1u", [D, DFF])
    w1d_d = din("w1d", [DFF, D])
    win_d = din("w_in", [D, INC])
    wout_d = din("w_out", [D, D])
    w2g_d = din("w2g", [D, DFF])
    w2u_d = din("w2u", [D, DFF])
    w2d_d = din("w2d", [DFF, D])
    out_d = nc.dram_tensor("out", [S, D], F32, kind="ExternalOutput")
    if dbg:
        dbg_d = nc.dram_tensor("dbg", [128, 128], F32, kind="ExternalOutput")
        dbgh_d = nc.dram_tensor("dbgh", [128, 4096], F32, kind="ExternalOutput")

    def dscr(name, shape):
        return nc.dram_tensor(name, shape, BF16, kind="Internal")

    sg = [dscr("sg1", [11, 128, 2048]), dscr("sg2", [11, 128, 2048])]
    su = [dscr("su1", [11, 128, 2048]), dscr("su2", [11, 128, 2048])]
    sd = [dscr("sd1", [16, 128, 1536]), dscr("sd2", [16, 128, 1536])]
    sin = dscr("sin", [12, 128, 2112])
    sout = dscr("sout", [4, 128, 2048])

    with ExitStack() as st:
        SC = Sched(nc, st)

        def sb(name, shape, dt):
            return st.enter_context(nc.sbuf_tensor(name, shape, dt))

        def psb(name, shape, dt):
            return st.enter_context(nc.psum_tensor(name, shape, dt))

        KT = sb("KT", [128, 4, S], BF16)
        Vc = sb("Vc", [128, 32, 4, 192], BF16)
        xT = sb("xT", [128, 8, TB], F32)
        hT = sb("hT", [128, 8, TB], BF16)
        yT = sb("yT", [128, 8, TB], BF16)
        actT = sb("actT", [128, 12, TB], BF16)
        qT = sb("qT", [128, 8, TB], BF16)
        ubuf = sb("ubuf", [128, 4, TB + 2], F32)
        NSLOT = 6
        DIRECT_FIRST = True
        ALWAYS_DIRECT = True
        NSTAGE = 1
        wsl = [sb(f"wsl{i}", [128, 2112], BF16) for i in range(NSLOT)]
        stage = Ring([sb(f"stage{i}", [128, D], F32) for i in range(NSTAGE)])
        ostage = Ring([sb(f"ostage{i}", [128, 512], F32) for i in range(2)])
        ftmp = Ring([sb(f"ftmp{i}", [128, TB], F32) for i in range(4)])
        btmp = Ring([sb(f"btmp{i}", [128, TB], BF16) for i in range(6)])
        rstd_ring = Ring([sb(f"rstd{i}", [128, TB], F32) for i in range(1)])
        gnr = Ring([sb(f"gnr{i}", [128, TB], F32) for i in range(1)])
        gsq = Ring([sb(f"gsq{i}", [128, TB], BF16) for i in range(3)])
        gsrc = Ring([sb(f"gsrc{i}", [128, TB], F32) for i in range(3)])
        ident_f = sb("ident_f", [128, 128], F32)
        ones_f = sb("ones_f", [128, 128], F32)
        tri_f = sb("tri_f", [128, 128], F32)
        ident_b = sb("ident_b", [128, 128], BF16)
        mean_b = sb("mean_b", [128, 128], BF16)
        blk_b = sb("blk_b", [128, 128], BF16)
        mask_b = sb("mask_b", [128, 128], BF16)
        vecs = sb("vecs_s", [128, 60], F32)
        ccol = sb("ccol_s", [128, 8], F32)
        scol = sb("scol", [128, 8], F32)
        adabc = sb("adabc_s", [128, 72], F32)
        modc = sb("modc", [128, 72], F32)
        acol = sb("acol", [128, 24], F32)
        gcol = sb("gcol", [128, 24], F32)
        Gall = sb("Gall", [128, 32, 8], F32)
        biasT = sb("biasT", [128, 32, 8], F32)
        carry = sb("carry", [128, 8], F32)
        gref = sb("gref", [128, 8], F32)
        fz = Ring([sb(f"fz{i}", [128, 8], F32) for i in range(4)])

        KTf = KT[:].rearrange("p a s -> p (a s)").bitcast(F32)
        ada_ring = Ring([KTf[:, i * 1024:(i + 1) * 1024] for i in range(8)])
        mod_ops = []
        pbanks = [psb(f"pb{i}", [128, 512], F32) for i in range(8)]
        PP = PsumPool(pbanks)

        R_xT = [Res() for _ in range(8)]
        R_hT = [Res() for _ in range(8)]
        R_yT = [Res() for _ in range(8)]
        R_act = [Res() for _ in range(12)]
        R_qT = [Res() for _ in range(8)]
        R_KT = [[Res() for _ in range(NB)] for _ in range(4)]
        R_V = [Res() for _ in range(32)]
        R_Vall = Res()
        R_u = [Res() for _ in range(4)]
        R_ws = [Res() for _ in range(NSLOT)]
        R_const = Res()
        R_cols = Res()
        R_G = Res()
        R_bias = Res()
        R_carry = Res()
        R_gref = Res()
        R_adaw = [Res() for _ in range(4)]

        ch_ws = [SC.chan() for _ in range(NSLOT)]
        ch_conv = [SC.chan() for _ in range(8)]
        ch_misc = [SC.chan() for _ in range(4)]
        ch_x = [SC.chan() for _ in range(2)]
        ch_out = [SC.chan() for _ in range(2)]
        ch_ada = [SC.chan() for _ in range(4)]

        R_sg = [[Res() for _ in range(11)] for _ in range(2)]
        R_su = [[Res() for _ in range(11)] for _ in range(2)]
        R_sd = [[Res() for _ in range(16)] for _ in range(2)]
        R_sin = [Res() for _ in range(12)]
        R_sout = [Res() for _ in range(4)]
        conv_i = [0]

        DIRECT = {}

        def conv(out_ap, in_ap, res, pat=None):
            if DIRECT_FIRST:
                DIRECT[id(res)] = (in_ap, pat)
                return
            c = ch_conv[conv_i[0] % len(ch_conv)]
            conv_i[0] += 1
            SC.dma("gpsimd", c, out_ap, in_ap, writes=[res])

        def conv_colunit(dst, src, c0, ncol, res, width):
            o = dst[:, 0:8 * ncol].rearrange("p (k f) -> p k f", k=8)
            i = src[:, c0:c0 + ncol].rearrange("(k p) f -> p k f", p=128)
            conv(o, i, res, ("p (k f) -> p k f", dict(k=8)))

        def conv_ffn(l, wg, wu, wd):
            for u in range(11):
                conv_colunit(sg[l][u], wg, u * 256, 256, R_sg[l][u], 2048)
                conv_colunit(su[l][u], wu, u * 256, 256, R_su[l][u], 2048)
            for h, (f0, nf) in enumerate(FH):
                for dc in range(8):
                    o = sd[l][h * 8 + dc][:, 0:nf * 128].rearrange("p (j d) -> p j d", j=nf)
                    i = wd[f0 * 128:(f0 + nf) * 128, dc * 128:(dc + 1) * 128].rearrange(
                        "(j p) d -> p j d", p=128)
                    conv(o, i, R_sd[l][h * 8 + dc], ("p (j d) -> p j d", dict(j=nf)))

        IN_UNITS = [(0, 256), (256, 256), (512, 256), (768, 256), (1024, 256), (1280, 264),
                    (1544, 256), (1800, 256), (2056, 256), (2312, 256), (2568, 256), (2824, 256)]

        def conv_all():
            conv_ffn(0, w1g_d, w1u_d, w1d_d)
            for u, (c0, ncol) in enumerate(IN_UNITS):
                conv_colunit(sin[u], win_d, c0, ncol, R_sin[u], 2112)
            for u in range(4):
                conv_colunit(sout[u], wout_d, u * 256, 256, R_sout[u], 2048)
            conv_ffn(1, w2g_d, w2u_d, w2d_d)

        ws_i = [0]

        def load_unit(src_ap, nelem, src_res):
            k = ws_i[0] % NSLOT
            ws_i[0] += 1
            if ALWAYS_DIRECT and id(src_res) in DIRECT:
                (in_ap, (pstr, pkw)) = DIRECT[id(src_res)]
                SC.dma("gpsimd", ch_ws[k], wsl[k][:, 0:nelem].rearrange(pstr, **pkw), in_ap, writes=[R_ws[k]])
            elif src_res.w is None and id(src_res) in DIRECT:
                (in_ap, (pstr, pkw)) = DIRECT[id(src_res)]
                SC.dma("gpsimd", ch_ws[k], wsl[k][:, 0:nelem].rearrange(pstr, **pkw), in_ap, writes=[R_ws[k]])
                c = ch_conv[conv_i[0] % len(ch_conv)]
                conv_i[0] += 1
                SC.dma("sync", c, src_ap, wsl[k][:, 0:nelem], reads=[R_ws[k]], writes=[src_res])
            else:
                SC.dma("sync", ch_ws[k], wsl[k][:, 0:nelem], src_ap, reads=[src_res], writes=[R_ws[k]])
            return wsl[k], R_ws[k]

        def setup():
            g = "gpsimd"
            SC.op(g, lambda e: e.memset(ident_f[:], 0.0), writes=[R_const])
            SC.op(g, lambda e: e.affine_select(ident_f[:], ident_f[:], [[-1, 128]], ALU.not_equal, 1.0,
                                               base=0, channel_multiplier=1), writes=[R_const])
            SC.op(g, lambda e: e.memset(ones_f[:], 1.0), writes=[R_const])
            SC.op(g, lambda e: e.memset(tri_f[:], 1.0), writes=[R_const])
            SC.op(g, lambda e: e.affine_select(tri_f[:], tri_f[:], [[1, 128]], ALU.is_ge, 0.0,
                                               base=0, channel_multiplier=-1), writes=[R_const])
            SC.op(g, lambda e: e.memset(ident_b[:], 0.0), writes=[R_const])
            SC.op(g, lambda e: e.affine_select(ident_b[:], ident_b[:], [[-1, 128]], ALU.not_equal, 1.0,
                                               base=0, channel_multiplier=1), writes=[R_const])
            SC.op(g, lambda e: e.memset(mean_b[:], 1.0 / 1024.0), writes=[R_const])
            SC.op(g, lambda e: e.memset(blk_b[:], 1.0 / 64.0), writes=[R_const])
            SC.op(g, lambda e: e.memset(blk_b[0:64, 64:128], 0.0), writes=[R_const])
            SC.op(g, lambda e: e.memset(blk_b[64:128, 0:64], 0.0), writes=[R_const])
            SC.op(g, lambda e: e.memset(mask_b[:], 0.0), writes=[R_const])
            SC.op(g, lambda e: e.affine_select(mask_b[:], mask_b[:], [[1, 128]], ALU.is_ge, NEG,
                                               base=0, channel_multiplier=-1), writes=[R_const])
            SC.op(g, lambda e: e.memset(carry[:], 0.0), writes=[R_carry])
            SC.op(g, lambda e: e.memset(ubuf[:], 0.0), writes=R_u)
            SC.op(g, lambda e: e.memset(qT[:], 0.0), writes=R_qT)
            SC.dma("sync", ch_misc[0], vecs[:], vecs_d.ap(), writes=[R_cols])
            SC.dma("sync", ch_misc[1], ccol[:], ccol_d.ap(), writes=[R_cols])

        def mod_setup():
            SC.op("scalar", lambda e: e.activation(scol[:], ccol[:], AF.Silu), reads=[R_cols], writes=[R_cols])
            SC.dma("sync", ch_misc[2], adabc[:], adabc_d.ap(), writes=[R_cols])
            SC.op("gpsimd", lambda e: e.memset(Vc[:, :, :, 64:128], 1.0), writes=[R_Vall] + R_V)

        ada_i = [0]

        def mod_vec(v):
            accs = [ftmp.get(), ftmp.get()]
            for kc in range(8):
                k = ada_i[0] % 4
                ada_i[0] += 1
                (awt, Rawt) = ada_ring.get()
                SC.dma("sync", ch_ada[k], awt,
                       adaw_d[kc * 128:(kc + 1) * 128, v * 1024:(v + 1) * 1024], writes=[Rawt])
                for hc in range(2):
                    (acc, Racc) = accs[hc]
                    if kc == 0:
                        mod_ops.append(SC.op("vector", lambda e, acc=acc, awt=awt, hc=hc, kc=kc: e.tensor_scalar(
                            acc[:], awt[:, hc * 512:(hc + 1) * 512], scol[:, kc:kc + 1], None, ALU.mult),
                            reads=[Rawt, R_cols], writes=[Racc]))
                    else:
                        mod_ops.append(SC.op("vector", lambda e, acc=acc, awt=awt, hc=hc, kc=kc: e.scalar_tensor_tensor(
                            acc[:], awt[:, hc * 512:(hc + 1) * 512], scol[:, kc:kc + 1], acc[:], ALU.mult, ALU.add),
                            reads=[Rawt, R_cols], writes=[Racc]))
            (pc, Rpc) = PP.alloc()
            for fcn in range(8):
                (acc, Racc) = accs[fcn // 4]
                SC.op("tensor", lambda e, pc=pc, acc=acc, fcn=fcn: e.matmul(
                    pc[:, fcn:fcn + 1], acc[:, (fcn % 4) * 128:(fcn % 4 + 1) * 128], ones_f[:, 0:1],
                    start=True, stop=True), reads=[Racc, R_const], writes=[Rpc])
            SC.op("vector", lambda e, pc=pc, v=v: e.tensor_tensor(
                modc[:, v * 8:(v + 1) * 8], pc[:, 0:8], adabc[:, v * 8:(v + 1) * 8], ALU.add),
                reads=[Rpc, R_cols], writes=[R_cols])
            PP.release((pc, Rpc))

        def mod_derive(l):
            sc_ = modc[:, (3 * l + 1) * 8:(3 * l + 2) * 8]
            ng = vecs[:, l * 8:(l + 1) * 8]
            SC.op("vector", lambda e, sc_=sc_, ng=ng, l=l: e.scalar_tensor_tensor(
                acol[:, l * 8:(l + 1) * 8], sc_, 1.0, ng, ALU.add, ALU.mult),
                reads=[R_cols], writes=[R_cols])

        def mod_gate(l):
            g_ = modc[:, (3 * l + 2) * 8:(3 * l + 3) * 8]
            fac = 1.0 if l == 1 else 0.5
            SC.op("vector", lambda e, g_=g_, l=l, fac=fac: e.tensor_scalar(
                gcol[:, l * 8:(l + 1) * 8], g_, fac, None, ALU.mult),
                reads=[R_cols], writes=[R_cols])

        def shcol(l, dc):
            return modc[:, 3 * l * 8 + dc:3 * l * 8 + dc + 1]

        xq = []
        actTf = actT[:].rearrange("p j t -> p (j t)").bitcast(F32)
        xtiles = [(stage.tiles[0][0][:], [stage.tiles[0][1]])]
        for k_ in range(3):
            xtiles.append((actTf[:, k_ * 1024:(k_ + 1) * 1024], R_act[4 * k_:4 * k_ + 4]))

        def prefetch_x(i, tts):
            for tt in tts:
                (stg, Rsl) = xtiles[tt]
                r0 = (4 * i + tt) * 128
                SC.dma("sync", ch_x[tt % 2], stg, x_d[r0:r0 + 128, :], writes=Rsl)
                xq.append((stg, Rsl))

        pre_tr = {}

        def load_transposes(stg, Rsl):
            banks = []
            for half in range(2):
                (pt, Rpt) = PP.alloc()
                for q in range(4):
                    dc = half * 4 + q
                    SC.op("tensor", lambda e, pt=pt, q=q, stg=stg, dc=dc: e.transpose(
                        pt[:, q * 128:(q + 1) * 128], stg[:, dc * 128:(dc + 1) * 128], ident_f[:]),
                        reads=list(Rsl) + [R_const], writes=[Rpt])
                banks.append((pt, Rpt))
            return banks

        def load_pre(i):
            (stg, Rsl) = xq.pop(0)
            pre_tr[i] = load_transposes(stg, Rsl)

        def load_block(i):
            for tt in range(4):
                if tt == 0 and i in pre_tr:
                    banks = pre_tr.pop(i)
                else:
                    (stg, Rsl) = xq.pop(0)
                    banks = load_transposes(stg, Rsl)
                for half, (pt, Rpt) in enumerate(banks):
                    eng = "scalar" if half == 0 else "vector"
                    o = xT[:, half * 4:half * 4 + 4, tt * 128:(tt + 1) * 128]
                    src = pt[:, :].rearrange("p (q t) -> p q t", q=4)
                    if eng == "scalar":
                        SC.op(eng, lambda e, o=o, src=src: e.copy(o, src), reads=[Rpt],
                              writes=R_xT[half * 4:half * 4 + 4])
                    else:
                        SC.op(eng, lambda e, o=o, src=src: e.tensor_copy(o, src), reads=[Rpt],
                              writes=R_xT[half * 4:half * 4 + 4])
                    PP.release((pt, Rpt))

        def norm_stats():
            (ps, Rps) = PP.alloc()
            for dc in range(8):
                (sq, Rsq) = btmp.get()
                SC.op("scalar", lambda e, sq=sq, dc=dc: e.activation(sq[:], xT[:, dc, :], AF.Square),
                      reads=[R_xT[dc]], writes=[Rsq])
                SC.op("tensor", lambda e, ps=ps, sq=sq, dc=dc: e.matmul(
                    ps[:], mean_b[:], sq[:], start=(dc == 0), stop=(dc == 7)),
                    reads=[Rsq, R_const], writes=[Rps])
            (rs, Rrs) = rstd_ring.get()
            SC.op("scalar", lambda e, rs=rs, ps=ps: e.activation(rs[:], ps[:], AF.Ln, bias=EPS, scale=1.0),
                  reads=[Rps], writes=[Rrs])
            PP.release((ps, Rps))
            SC.op("scalar", lambda e, rs=rs: e.activation(rs[:], rs[:], AF.Exp, scale=-0.5),
                  reads=[], writes=[Rrs])
            return rs, Rrs

        def norm_mod(l):
            rs, Rrs = norm_stats()
            for dc in range(8):
                (t, Rt) = ftmp.get()
                SC.op("vector", lambda e, t=t, dc=dc, rs=rs, l=l: e.scalar_tensor_tensor(
                    t[:], xT[:, dc, :], acol[:, l * 8 + dc:l * 8 + dc + 1], rs[:], ALU.mult, ALU.mult),
                    reads=[R_xT[dc], Rrs, R_cols], writes=[Rt])
                SC.op("scalar", lambda e, t=t, dc=dc, l=l: e.activation(
                    hT[:, dc, :], t[:], AF.Identity, bias=shcol(l, dc), scale=1.0),
                    reads=[Rt, R_cols], writes=[R_hT[dc]])

        def ffn(l, gl, hook=None):
            for h, (f0, nf) in enumerate(FH):
                units = {}

                def unit(u):
                    if u not in units:
                        gs, Rgs = load_unit(sg[l][u], 2048, R_sg[l][u])
                        us, Rus = load_unit(su[l][u], 2048, R_su[l][u])
                        units[u] = (gs[:, 0:2048].rearrange("p (k f) -> p k f", k=8), Rgs,
                                    us[:, 0:2048].rearrange("p (k f) -> p k f", k=8), Rus)
                    return units[u]

                groups = [[f0, f0 + 1, f0 + 2]] + [[fc] for fc in range(f0 + 3, f0 + nf)] if h == 0 \
                    else [[fc] for fc in range(f0, f0 + nf)]
                for grp in groups:
                    info = []
                    for fc in grp:
                        gv, Rgs, uv, Rus = unit(fc // 2)
                        info.append((fc, gv, Rgs, uv, Rus, PP.alloc(), PP.alloc()))
                    if len(grp) == 1:
                        order = [(it, w, kc) for it in info for w in (0, 1) for kc in range(8)]
                    else:
                        order = [(it, w, kc) for kc in range(8) for it in info for w in (0, 1)]
                    for (it, w, kc) in order:
                        (fc, gv, Rgs, uv, Rus, (pg, Rpg), (pu, Rpu)) = it
                        s0 = (fc % 2) * 128
                        if w == 0:
                            SC.op("tensor", lambda e, pg=pg, gv=gv, kc=kc, s0=s0: e.matmul(
                                pg[:], gv[:, kc, s0:s0 + 128], hT[:, kc, :], start=(kc == 0), stop=(kc == 7)),
                                reads=[Rgs, R_hT[kc]], writes=[Rpg])
                        else:
                            SC.op("tensor", lambda e, pu=pu, uv=uv, kc=kc, s0=s0: e.matmul(
                                pu[:], uv[:, kc, s0:s0 + 128], hT[:, kc, :], start=(kc == 0), stop=(kc == 7)),
                                reads=[Rus, R_hT[kc]], writes=[Rpu])
                    for it in info:
                        (fc, gv, Rgs, uv, Rus, (pg, Rpg), (pu, Rpu)) = it
                        j = fc - f0
                        (sl, Rsl) = btmp.get()
                        SC.op("scalar", lambda e, sl=sl, pg=pg: e.activation(sl[:], pg[:], AF.Silu),
                              reads=[Rpg], writes=[Rsl])
                        PP.release((pg, Rpg))
                        SC.op("vector", lambda e, j=j, pu=pu, sl=sl: e.tensor_tensor(
                            actT[:, j, :], pu[:], sl[:], ALU.mult), reads=[Rpu, Rsl], writes=[R_act[j]])
                        PP.release((pu, Rpu))
                if hook is not None:
                    hook(h)
                for dc in range(8):
                    ds, Rds = load_unit(sd[l][h * 8 + dc][:, 0:nf * 128], nf * 128, R_sd[l][h * 8 + dc])
                    dv = ds[:, 0:nf * 128].rearrange("p (j d) -> p j d", j=nf)
                    (py, Rpy) = PP.alloc()
                    for j in range(nf):
                        SC.op("tensor", lambda e, py=py, dv=dv, j=j, nf=nf: e.matmul(
                            py[:], dv[:, j, :], actT[:, j, :], start=(j == 0), stop=(j == nf - 1)),
                            reads=[Rds, R_act[j]], writes=[Rpy])
                    SC.op("vector", lambda e, py=py, dc=dc, gl=gl: e.scalar_tensor_tensor(
                        xT[:, dc, :], py[:], gcol[:, gl * 8 + dc:gl * 8 + dc + 1], xT[:, dc, :],
                        ALU.mult, ALU.add), reads=[Rpy, R_cols], writes=[R_xT[dc]])
                    PP.release((py, Rpy))

        gn_pending = []

        def groupnorm_to_y(src, Rsrc, kc):
            gn_pending.append((src, Rsrc, kc))

        def gn_flush(keep=0):
            while len(gn_pending) > keep:
                (src, Rsrc, kc) = gn_pending.pop(0)
                (sq, Rsq) = gsq.get()
                SC.op("scalar", lambda e, sq=sq, src=src: e.activation(sq[:], src[:], AF.Square),
                      reads=[Rsrc], writes=[Rsq])
                (ps, Rps) = PP.alloc()
                SC.op("tensor", lambda e, ps=ps, sq=sq: e.matmul(ps[:], blk_b[:], sq[:], start=True, stop=True),
                      reads=[Rsq, R_const], writes=[Rps])
                (rs, Rrs) = gnr.get()
                SC.op("scalar", lambda e, rs=rs, ps=ps: e.activation(rs[:], ps[:], AF.Ln, bias=EPS, scale=1.0),
                      reads=[Rps], writes=[Rrs])
                PP.release((ps, Rps))
                SC.op("scalar", lambda e, rs=rs: e.activation(rs[:], rs[:], AF.Exp, scale=-0.5),
                      reads=[], writes=[Rrs])
                SC.op("vector", lambda e, src=src, rs=rs, kc=kc: e.scalar_tensor_tensor(
                    yT[:, kc, :], src[:], vecs[:, 32 + kc:33 + kc], rs[:], ALU.mult, ALU.mult),
                    reads=[Rsrc, Rrs, R_cols], writes=[R_yT[kc]])

        def in_proj(i):
            SC.op("vector", lambda e: e.tensor_copy(gref[:], carry[:]), reads=[R_carry], writes=[R_gref])
            for up in range(2):
                cs, Rc = load_unit(sin[8 + up][:, 0:2048], 2048, R_sin[8 + up])
                xs, Rx = load_unit(sin[10 + up][:, 0:2048], 2048, R_sin[10 + up])
                bs, Rb = load_unit(sin[6 + up][:, 0:2048], 2048, R_sin[6 + up])
                bv = bs[:, 0:2048].rearrange("p (k f) -> p k f", k=8)
                cv_ = cs[:, 0:2048].rearrange("p (k f) -> p k f", k=8)
                xv = xs[:, 0:2048].rearrange("p (k f) -> p k f", k=8)
                for s_ in range(2):
                    cc = up * 2 + s_
                    outs = [PP.alloc() for _ in range(3)]
                    for kc in range(8):
                        for (wv, Rw), (pp, Rpp) in zip(((cv_, Rc), (xv, Rx), (bv, Rb)), outs):
                            SC.op("tensor", lambda e, pp=pp, wv=wv, kc=kc, s_=s_: e.matmul(
                                pp[:], wv[:, kc, s_ * 128:(s_ + 1) * 128], hT[:, kc, :],
                                start=(kc == 0), stop=(kc == 7)), reads=[Rw, R_hT[kc]], writes=[Rpp])
                    (pC, RpC), (pX, RpX), (pB, RpB) = outs
                    gn_flush(keep=1)
                    (csb, Rcsb) = ftmp.get()
                    SC.op("scalar", lambda e, csb=csb, pC=pC: e.copy(csb[:], pC[:]), reads=[RpC], writes=[Rcsb])
                    PP.release((pC, RpC))
                    SC.op("vector", lambda e, cc=cc, pX=pX, csb=csb: e.tensor_tensor(
                        ubuf[:, cc, 2:TB + 2], pX[:], csb[:], ALU.mult), reads=[RpX, Rcsb], writes=[R_u[cc]])
                    PP.release((pX, RpX))
                    (bsb, Rbsb) = gsrc.get()
                    SC.op("scalar", lambda e, bsb=bsb, pB=pB: e.copy(bsb[:], pB[:]), reads=[RpB], writes=[Rbsb])
                    PP.release((pB, RpB))
                    w0 = vecs[:, 40 + cc:41 + cc]
                    w1 = vecs[:, 44 + cc:45 + cc]
                    w2 = vecs[:, 48 + cc:49 + cc]
                    (y1, Ry1) = ftmp.get()
                    SC.op("scalar", lambda e, y1=y1, cc=cc, w2=w2: e.activation(
                        y1[:], ubuf[:, cc, 2:TB + 2], AF.Identity, scale=w2), reads=[R_u[cc], R_cols], writes=[Ry1])
                    SC.op("vector", lambda e, y1=y1, cc=cc, w1=w1: e.scalar_tensor_tensor(
                        y1[:], ubuf[:, cc, 1:TB + 1], w1, y1[:], ALU.mult, ALU.add),
                        reads=[R_u[cc], R_cols], writes=[Ry1])
                    SC.op("vector", lambda e, y1=y1, cc=cc, w0=w0: e.scalar_tensor_tensor(
                        y1[:], ubuf[:, cc, 0:TB], w0, y1[:], ALU.mult, ALU.add),
                        reads=[R_u[cc], R_cols], writes=[Ry1])
                    SC.op("vector", lambda e, cc=cc: e.tensor_copy(ubuf[:, cc, 0:2], ubuf[:, cc, TB:TB + 2]),
                          reads=[], writes=[R_u[cc]])
                    SC.op("vector", lambda e, bsb=bsb, y1=y1: e.tensor_tensor(
                        bsb[:], bsb[:], y1[:], ALU.mult), reads=[Ry1], writes=[Rbsb])
                    groupnorm_to_y(bsb, Rbsb, 4 + cc)
            for u in range(4):
                ws, Rw = load_unit(sin[u][:, 0:2048], 2048, R_sin[u])
                wv = ws[:, 0:2048].rearrange("p (k f) -> p k f", k=8)
                for s_ in range(2):
                    j = (u % 2) * 2 + s_
                    (pq, Rpq) = PP.alloc()
                    for kc in range(8):
                        SC.op("tensor", lambda e, pq=pq, wv=wv, kc=kc, s_=s_: e.matmul(
                            pq[:], wv[:, kc, s_ * 128:(s_ + 1) * 128], hT[:, kc, :],
                            start=(kc == 0), stop=(kc == 7)), reads=[Rw, R_hT[kc]], writes=[Rpq])
                    if u < 2:
                        SC.op("scalar", lambda e, pq=pq, j=j: e.copy(qT[0:64, 2 * j, :], pq[0:64, :]),
                              reads=[Rpq], writes=[R_qT[2 * j]])
                        SC.op("scalar", lambda e, pq=pq, j=j: e.copy(qT[64:128, 2 * j + 1, :], pq[64:128, :]),
                              reads=[Rpq], writes=[R_qT[2 * j + 1]])
                    else:
                        SC.op("vector", lambda e, pq=pq, j=j, i=i: e.tensor_copy(
                            KT[:, j, i * TB:(i + 1) * TB], pq[:]), reads=[Rpq], writes=[R_KT[j][i]],
                            extra=(mod_ops[-1:] if i == 0 else ()))
                    PP.release((pq, Rpq))
            gn_flush()
            v0, Rv0 = load_unit(sin[4][:, 0:2048], 2048, R_sin[4])
            v1, Rv1 = load_unit(sin[5][:, 0:2112], 2112, R_sin[5])
            v0v = v0[:, 0:2048].rearrange("p (k f) -> p k f", k=8)
            v1v = v1[:, 0:2112].rearrange("p (k f) -> p k f", k=8)
            for tt in range(4):
                kb = 4 * i + tt
                (pv0, Rpv0) = PP.alloc()
                (pv1, Rpv1) = PP.alloc()
                for kc in range(8):
                    SC.op("tensor", lambda e, pv0=pv0, kc=kc, tt=tt: e.matmul(
                        pv0[:, 0:256], hT[:, kc, tt * 128:(tt + 1) * 128], v0v[:, kc, :],
                        start=(kc == 0), stop=(kc == 7)), reads=[Rv0, R_hT[kc]], writes=[Rpv0])
                for kc in range(8):
                    SC.op("tensor", lambda e, pv1=pv1, kc=kc, tt=tt: e.matmul(
                        pv1[:, 0:264], hT[:, kc, tt * 128:(tt + 1) * 128], v1v[:, kc, :],
                        start=(kc == 0), stop=(kc == 7)), reads=[Rv1, R_hT[kc]], writes=[Rpv1])
                (z, Rz) = fz.get()
                SC.op("vector", lambda e, z=z, pv1=pv1: e.tensor_tensor(
                    z[:], pv1[:, 256:264], vecs[:, 52:60], ALU.add), reads=[Rpv1, R_cols], writes=[Rz])
                for g_, (pvx, Rpvx) in enumerate(((pv0, Rpv0), (pv1, Rpv1))):
                    src = pvx[:, 0:256].rearrange("p (a b d) -> p a b d", a=2, b=2)
                    dst = Vc[:, kb, 2 * g_:2 * g_ + 2, :].rearrange("p a (b d) -> p a b d", b=3)[:, :, 0:3:2, :]
                    if g_ == 0:
                        SC.op("scalar", lambda e, dst=dst, src=src: e.copy(dst, src),
                              reads=[Rpvx], writes=[R_V[kb]])
                    else:
                        SC.op("vector", lambda e, dst=dst, src=src: e.tensor_copy(dst, src),
                              reads=[Rpvx], writes=[R_V[kb]])
                PP.release((pv0, Rpv0))
                PP.release((pv1, Rpv1))
                (ez, Rez) = fz.get()
                SC.op("scalar", lambda e, ez=ez, z=z: e.activation(ez[:], z[:], AF.Exp, scale=-1.0),
                      reads=[Rz], writes=[Rez])
                (sp, Rsp) = fz.get()
                SC.op("scalar", lambda e, sp=sp, ez=ez: e.activation(sp[:], ez[:], AF.Ln, bias=1.0, scale=1.0),
                      reads=[Rez], writes=[Rsp])
                (pf, Rpf) = PP.alloc()
                SC.op("tensor", lambda e, pf=pf, sp=sp: e.matmul(pf[:, 0:8], tri_f[:], sp[:], start=True, stop=True),
                      reads=[Rsp, R_const], writes=[Rpf])
                SC.op("tensor", lambda e, pf=pf, sp=sp: e.matmul(pf[:, 8:16], ones_f[:], sp[:], start=True, stop=True),
                      reads=[Rsp, R_const], writes=[Rpf])
                SC.op("vector", lambda e, pf=pf, kb=kb: e.tensor_tensor(
                    Gall[:, kb, :], pf[:, 0:8], carry[:], ALU.add), reads=[Rpf, R_carry], writes=[R_G])
                SC.op("vector", lambda e, pf=pf: e.tensor_tensor(
                    carry[:], pf[:, 8:16], carry[:], ALU.add), reads=[Rpf], writes=[R_carry])
                PP.release((pf, Rpf))

        LA = 3
        FASTRECIP = False

        def attention(i):
            nkb = 4 * i + 4
            SC.op("vector", lambda e: e.tensor_tensor(gref[:], gref[:], carry[:], ALU.add),
                  reads=[R_carry], writes=[R_gref])
            SC.op("vector", lambda e: e.tensor_scalar(gref[:], gref[:], 0.5, None, ALU.mult),
                  reads=[], writes=[R_gref])
            for h in range(8):
                SC.op("vector", lambda e, h=h, nkb=nkb: e.tensor_scalar(
                    biasT[:, 0:nkb, h], Gall[:, 0:nkb, h], gref[:, h:h + 1], None, ALU.subtract),
                    reads=[R_G, R_gref], writes=[R_bias])
            steps = [(h, kb) for h in range(8) for kb in range(nkb)]
            sbank = {}
            fin_pending = []

            def emit_qk(si):
                h, kb = steps[si]
                pr, half = h // 2, h % 2
                b0 = 64 * half
                dq = kb - 4 * i
                n0 = max(0, dq) * 128
                (ps, Rps) = PP.alloc()
                sbank[si] = (ps, Rps)
                SC.op("tensor", lambda e, ps=ps, pr=pr, kb=kb, h=h, n0=n0, dq=dq: e.matmul(
                    ps[:, n0:TB], KT[:, pr, kb * 128:(kb + 1) * 128], qT[:, h, n0:TB],
                    start=True, stop=(dq < 0)),
                    reads=[R_KT[pr][kb // 4], R_qT[h]], writes=[Rps])
                if dq >= 0:
                    SC.op("tensor", lambda e, ps=ps, n0=n0: e.matmul(
                        ps[:, n0:n0 + 128], ident_b[:], mask_b[:], start=False, stop=True),
                        reads=[R_const], writes=[Rps])

            for si in range(min(LA, len(steps))):
                emit_qk(si)
            cur = {}
            for si, (h, kb) in enumerate(steps):
                pr, half = h // 2, h % 2
                b0 = 64 * half
                dq = kb - 4 * i
                n0 = max(0, dq) * 128
                if si + LA < len(steps):
                    emit_qk(si + LA)
                if kb == 0:
                    cur["po"] = PP.alloc()
                    if half == 0:
                        cur["onp"] = gsrc.get()
                if kb == min(10, nkb - 1) and half == 0:
                    gn_flush()
                (po, Rpo) = cur["po"]
                (onp, Ronp) = cur["onp"]
                (ps, Rps) = sbank.pop(si)
                (pt, Rpt) = btmp.get()
                SC.op("scalar", lambda e, pt=pt, ps=ps, n0=n0, kb=kb, h=h: e.activation(
                    pt[:, n0:TB], ps[:, n0:TB], AF.Exp, bias=biasT[:, kb, h:h + 1], scale=0.125),
                    reads=[Rps, R_bias], writes=[Rpt])
                PP.release((ps, Rps))
                off = 64 * half
                SC.op("tensor", lambda e, po=po, pt=pt, kb=kb, pr=pr, off=off, n0=n0, nkb=nkb: e.matmul(
                    po[:, n0:TB], Vc[:, kb, pr, off:off + 128], pt[:, n0:TB],
                    start=(kb == 0), stop=(kb == nkb - 1)),
                    reads=[R_V[kb], Rpt], writes=[Rpo])
                while fin_pending and fin_pending[0][0] <= si:
                    fin_pending.pop(0)[1]()
                if kb == nkb - 1:
                    def finalize(po=po, Rpo=Rpo, onp=onp, Ronp=Ronp, b0=b0, half=half, pr=pr):
                        d0 = 64 - b0
                        (osb, Rosb) = ftmp.get()
                        SC.op("scalar", lambda e: e.copy(osb[:], po[:]), reads=[Rpo], writes=[Rosb])
                        PP.release((po, Rpo))
                        (rc, Rrc) = ftmp.get()
                        if FASTRECIP:
                            SC.op("vector", lambda e: e.reciprocal_approx_fast(
                                rc[b0:b0 + 64, :], osb[d0:d0 + 64, :]), reads=[Rosb], writes=[Rrc])
                        else:
                            SC.op("vector", lambda e: e.reciprocal(
                                rc[b0:b0 + 64, :], osb[d0:d0 + 64, :]), reads=[Rosb], writes=[Rrc])
                        SC.op("vector", lambda e: e.tensor_tensor(
                            onp[b0:b0 + 64, :], osb[b0:b0 + 64, :], rc[b0:b0 + 64, :], ALU.mult),
                            reads=[Rosb, Rrc], writes=[Ronp])
                        if half == 1:
                            groupnorm_to_y(onp, Ronp, pr)
                    fin_pending.append((si + 2, finalize))
            while fin_pending:
                fin_pending.pop(0)[1]()

        def out_proj():
            units = {}

            def unit(u):
                if u not in units:
                    ws, Rw = load_unit(sout[u], 2048, R_sout[u])
                    units[u] = (ws[:, 0:2048].rearrange("p (k f) -> p k f", k=8), Rw)
                return units[u]

            def mm(py, Rpy, dc, kcs, first, last):
                wv, Rw = unit(dc // 2)
                s_ = dc % 2
                for n_, kc in enumerate(kcs):
                    SC.op("tensor", lambda e, py=py, wv=wv, kc=kc, s_=s_, n_=n_: e.matmul(
                        py[:], wv[:, kc, s_ * 128:(s_ + 1) * 128], yT[:, kc, :],
                        start=(first and n_ == 0), stop=(last and n_ == len(kcs) - 1)),
                        reads=[Rw, R_yT[kc]], writes=[Rpy])

            def evac(py, Rpy, dc):
                SC.op("vector", lambda e, py=py, dc=dc: e.scalar_tensor_tensor(
                    xT[:, dc, :], py[:], gcol[:, 8 + dc:9 + dc], xT[:, dc, :], ALU.mult, ALU.add),
                    reads=[Rpy, R_cols], writes=[R_xT[dc]])
                PP.release((py, Rpy))

            held = []
            for dc in range(6):
                (py, Rpy) = PP.alloc()
                mm(py, Rpy, dc, (4, 5, 6, 7), True, False)
                held.append((py, Rpy, dc))
            gn_flush()
            for (py, Rpy, dc) in held:
                mm(py, Rpy, dc, (0, 1, 2, 3), False, True)
                evac(py, Rpy, dc)
            for dc in (6, 7):
                (py, Rpy) = PP.alloc()
                mm(py, Rpy, dc, (4, 5, 6, 7, 0, 1, 2, 3), True, True)
                evac(py, Rpy, dc)

        store_ops = []

        def final_store(i):
            if fnorm:
                rs, Rrs = norm_stats()
                for dc in range(8):
                    SC.op("vector", lambda e, dc=dc, rs=rs: e.scalar_tensor_tensor(
                        xT[:, dc, :], xT[:, dc, :], vecs[:, 24 + dc:25 + dc], rs[:], ALU.mult, ALU.mult),
                        reads=[Rrs, R_cols], writes=[R_xT[dc]])
            for tt in range(4):
                r0 = (4 * i + tt) * 128
                for half in range(2):
                    (stg, Rs) = ostage.get()
                    (pt, Rpt) = PP.alloc()
                    for q in range(4):
                        dc = half * 4 + q
                        SC.op("tensor", lambda e, pt=pt, q=q, dc=dc, tt=tt: e.transpose(
                            pt[:, q * 128:(q + 1) * 128], xT[:, dc, tt * 128:(tt + 1) * 128], ident_f[:]),
                            reads=[R_xT[dc], R_const], writes=[Rpt])
                    if half == 0:
                        SC.op("scalar", lambda e, stg=stg, pt=pt: e.copy(stg[:], pt[:]),
                              reads=[Rpt], writes=[Rs])
                    else:
                        SC.op("vector", lambda e, stg=stg, pt=pt: e.tensor_copy(stg[:], pt[:]),
                              reads=[Rpt], writes=[Rs])
                    PP.release((pt, Rpt))
                    o = SC.dma("gpsimd", ch_out[half], out_d[r0:r0 + 128, half * 512:(half + 1) * 512], stg[:],
                               reads=[Rs])
                    store_ops.append(o)

        setup()
        mod_setup()
        mod_vec(0)
        mod_vec(1)
        mod_derive(0)
        conv_all()
        prefetch_x(0, [0, 1, 2, 3])
        for i in range(nblocks):
            load_block(i)
            if upto >= 1:
                norm_mod(0)
                if dbg and i == 0:
                    chd = SC.chan()
                    store_ops.append(SC.dma("gpsimd", chd, dbg_d[:, 0:72], modc[:], reads=[R_cols]))
                    store_ops.append(SC.dma("gpsimd", chd, dbg_d[:, 72:96], acol[:], reads=[R_cols]))
                    store_ops.append(SC.dma("gpsimd", chd, dbg_d[:, 96:120], gcol[:], reads=[R_cols]))
                    store_ops.append(SC.dma("gpsimd", chd, dbgh_d.ap().rearrange("p (k t) -> p k t", k=8), hT[:], reads=R_hT))
                if i == 0:
                    def hook(h):
                        if h == 0:
                            mod_vec(2)
                            mod_gate(0)
                            mod_vec(3)
                            mod_vec(4)
                            mod_derive(1)
                        else:
                            mod_vec(5)
                            mod_gate(1)
                            mod_vec(6)
                            mod_vec(7)
                            mod_derive(2)
                            mod_vec(8)
                            mod_gate(2)
                    ffn(0, 0, hook)
                else:
                    ffn(0, 0)
            if upto >= 2:
                norm_mod(1)
                in_proj(i)
                attention(i)
                out_proj()
            if i + 1 < nblocks:
                prefetch_x(i + 1, [0])
            if upto >= 3:
                norm_mod(2)
                ffn(1, 2)
            if i + 1 < nblocks:
                prefetch_x(i + 1, [1, 2, 3])
                load_pre(i + 1)
            final_store(i)
        SC.op("gpsimd", None, extra=store_ops[-12:])
        SC.emit()
    return nc


_CACHE = {}


def _col(v):
    return np.ascontiguousarray(np.asarray(v, np.float32).reshape(-1, 128).T)


def kernel(x, c, ada_w, ada_b, norm1_g, ffn1_w_gate, ffn1_w_up, ffn1_w_down,
           norm2_g, w_in, forget_bias, conv_w, group_norm_g, w_out,
           norm3_g, ffn2_w_gate, ffn2_w_up, ffn2_w_down, final_g):
    f32 = lambda a: np.ascontiguousarray(np.asarray(a, dtype=np.float32))
    x = f32(x)
    c = f32(c)
    vecs = np.zeros((128, 60), np.float32)
    vecs[:, 0:8] = _col(norm1_g)
    vecs[:, 8:16] = _col(norm2_g)
    vecs[:, 16:24] = _col(norm3_g)
    vecs[:, 24:32] = _col(final_g)
    vecs[:, 32:40] = _col(group_norm_g)
    cw = f32(conv_w)
    for j in range(3):
        vecs[:, 40 + 4 * j:44 + 4 * j] = _col(cw[j])
    vecs[:, 52:60] = np.broadcast_to(f32(forget_bias)[None, :], (128, 8))
    shared = {
        "ada_w": f32(ada_w), "adabc": _col(ada_b), "vecs": vecs,
        "w1g": f32(ffn1_w_gate), "w1u": f32(ffn1_w_up), "w1d": f32(ffn1_w_down),
        "w_in": f32(w_in), "w_out": f32(w_out),
        "w2g": f32(ffn2_w_gate), "w2u": f32(ffn2_w_up), "w2d": f32(ffn2_w_down),
    }
    in_maps = []
    for b in range(8):
        m = dict(shared)
        m["x"] = x[b]
        m["ccol"] = _col(c[b])
        in_maps.append(m)
    if "nc" not in _CACHE:
        _CACHE["nc"] = build_program()
    nc = _CACHE["nc"]
    res = run_bass_kernel_spmd(nc, in_maps, core_ids=list(range(8)))
    return np.stack([np.asarray(r["out"], dtype=np.float32) for r in res.results], axis=0)
```

```python
from contextlib import ExitStack
import numpy as np
import concourse.bass as bass
import concourse.mybir as mybir
from concourse.bass_utils import run_bass_kernel_spmd

F32 = mybir.dt.float32
BF16 = mybir.dt.bfloat16
AF = mybir.ActivationFunctionType
ALU = mybir.AluOpType

D = 1024
S = 4096
DFF = 2816
NFC = 22
INC = 3080
NB = 8
TB = 512
EPS = 1e-6
NEG = -30000.0
FH = ((0, 12), (12, 10))

ENG_NAMES = ("tensor", "vector", "scalar", "gpsimd", "sync")
SEM_ROLL = 30000


class Op:
    __slots__ = ("eng", "fn", "deps", "signal", "sem", "count", "is_dma")

    def __init__(self, eng, fn, deps, is_dma=False):
        self.eng = eng
        self.fn = fn
        self.deps = deps
        self.signal = False
        self.sem = None
        self.count = 0
        self.is_dma = is_dma


class Chan:
    def __init__(self, sem):
        self.sem = sem
        self.count = 0
        self.last = None


class Res:
    __slots__ = ("w", "r", "rd", "excl")

    def __init__(self, excl=False):
        self.w = None
        self.r = {}
        self.rd = []
        self.excl = excl


class Sched:
    def __init__(self, nc, stack):
        self.nc = nc
        self.stack = stack
        self.ops = {e: [] for e in ENG_NAMES}
        self.nsem = 0

    def new_sem(self, name=None):
        self.nsem += 1
        return self.stack.enter_context(self.nc.semaphore(name or f"s{self.nsem}"))

    def chan(self):
        return Chan(self.new_sem())

    def _deps(self, eng, reads, writes, extra, is_dma):
        deps = []
        wr = list(writes)
        for R in reads:
            if R.excl:
                wr.append(R)
            elif R.w is not None:
                deps.append(R.w)
        for R in wr:
            if R.w is not None:
                deps.append(R.w)
            deps.extend(R.r.values())
            deps.extend(R.rd)
        deps.extend(d for d in extra if d is not None)
        out = []
        seen = set()
        for d in deps:
            if id(d) in seen:
                continue
            seen.add(id(d))
            if (not is_dma) and (not d.is_dma) and d.eng == "tensor" and eng == "tensor":
                continue
            out.append(d)
        return out

    def _update(self, o, reads, writes):
        for R in reads:
            if R.excl:
                R.w = o
                R.r = {}
                R.rd = []
            elif o.is_dma:
                R.rd.append(o)
            else:
                R.r[o.eng] = o
        for R in writes:
            R.w = o
            R.r = {}
            R.rd = []

    def op(self, eng, fn, reads=(), writes=(), extra=()):
        deps = self._deps(eng, reads, writes, extra, False)
        o = Op(eng, fn, deps)
        for d in deps:
            d.signal = True
        self.ops[eng].append(o)
        self._update(o, reads, writes)
        return o

    def dma(self, eng, chan, out, in_, reads=(), writes=(), extra=()):
        deps = self._deps(eng, reads, writes, list(extra) + [chan.last], True)

        def fn(e):
            return e.dma_start(out=out, in_=in_)

        o = Op(eng, fn, deps, is_dma=True)
        for d in deps:
            d.signal = True
        chan.count += 16
        chan.last = o
        o.sem = chan.sem
        o.count = chan.count
        self.ops[eng].append(o)
        self._update(o, reads, writes)
        return o

    def emit(self):
        nc = self.nc
        for e in ENG_NAMES:
            sem = None
            cnt = 0
            for o in self.ops[e]:
                if o.is_dma:
                    continue
                if o.signal:
                    if o.fn is None:
                        raise RuntimeError("wait-only op cannot signal")
                    if sem is None or cnt >= SEM_ROLL:
                        sem = self.new_sem()
                        cnt = 0
                    cnt += 1
                    o.sem = sem
                    o.count = cnt
        with nc.Block() as block:
            for e in ENG_NAMES:
                ops = self.ops[e]
                if not ops:
                    continue

                def body(engh, ops=ops):
                    waited = {}
                    for o in ops:
                        for d in o.deps:
                            key = id(d.sem)
                            if waited.get(key, 0) >= d.count:
                                continue
                            engh.wait_ge(d.sem, d.count)
                            waited[key] = d.count
                        if o.fn is None:
                            continue
                        ins = o.fn(engh)
                        if o.is_dma:
                            ins.then_inc(o.sem, 16)
                        elif o.signal:
                            ins.then_inc(o.sem, 1)

                getattr(block, e)(body)


class Ring:
    def __init__(self, tiles):
        self.tiles = [(t, Res()) for t in tiles]
        self.i = 0

    def get(self):
        t = self.tiles[self.i % len(self.tiles)]
        self.i += 1
        return t


class PsumPool:
    def __init__(self, banks):
        self.free = [(b, Res(excl=True)) for b in banks]

    def alloc(self):
        if not self.free:
            raise RuntimeError("psum pool exhausted")
        return self.free.pop(0)

    def release(self, b):
        self.free.append(b)


def build_program(nblocks=NB, upto=99, fnorm=True, dbg=False):
    nc = bass.Bass("TRN2", target_bir_lowering=False)

    def din(name, shape, dt=F32):
        return nc.dram_tensor(name, shape, dt, kind="ExternalInput")

    x_d = din("x", [S, D])
    ccol_d = din("ccol", [128, 8])
    adaw_d = din("ada_w", [D, 9 * D])
    vecs_d = din("vecs", [128, 60])
    adabc_d = din("adabc", [128, 72])
    w1g_d = din("w1g", [D, DFF])
    w1u_d = din("w1u", [D, DFF])
    w1d_d = din("w1d", [DFF, D])
    win_d = din("w_in", [D, INC])
    wout_d = din("w_out", [D, D])
    w2g_d = din("w2g", [D, DFF])
    w2u_d = din("w2u", [D, DFF])
    w2d_d = din("w2d", [DFF, D])
    out_d = nc.dram_tensor("out", [S, D], F32, kind="ExternalOutput")
    if dbg:
        dbg_d = nc.dram_tensor("dbg", [128, 128], F32, kind="ExternalOutput")
        dbgh_d = nc.dram_tensor("dbgh", [128, 4096], F32, kind="ExternalOutput")

    def dscr(name, shape):
        return nc.dram_tensor(name, shape, BF16, kind="Internal")

    sg = [dscr("sg1", [11, 128, 2048]), dscr("sg2", [11, 128, 2048])]
    su = [dscr("su1", [11, 128, 2048]), dscr("su2", [11, 128, 2048])]
    sd = [dscr("sd1", [16, 128, 1536]), dscr("sd2", [16, 128, 1536])]
    sin = dscr("sin", [12, 128, 2112])
    sout = dscr("sout", [4, 128, 2048])

    with ExitStack() as st:
        SC = Sched(nc, st)

        def sb(name, shape, dt):
            return st.enter_context(nc.sbuf_tensor(name, shape, dt))

        def psb(name, shape, dt):
            return st.enter_context(nc.psum_tensor(name, shape, dt))

        KT = sb("KT", [128, 4, S], BF16)
        Vc = sb("Vc", [128, 32, 4, 192], BF16)
        xT = sb("xT", [128, 8, TB], F32)
        hT = sb("hT", [128, 8, TB], BF16)
        yT = sb("yT", [128, 8, TB], BF16)
        actT = sb("actT", [128, 12, TB], BF16)
        qT = sb("qT", [128, 8, TB], BF16)
        ubuf = sb("ubuf", [128, 4, TB + 2], F32)
        NSLOT = 6
        DIRECT_FIRST = True
        ALWAYS_DIRECT = True
        NSTAGE = 1
        wsl = [sb(f"wsl{i}", [128, 2112], BF16) for i in range(NSLOT)]
        stage = Ring([sb(f"stage{i}", [128, D], F32) for i in range(NSTAGE)])
        ostage = Ring([sb(f"ostage{i}", [128, 512], F32) for i in range(2)])
        ftmp = Ring([sb(f"ftmp{i}", [128, TB], F32) for i in range(4)])
        btmp = Ring([sb(f"btmp{i}", [128, TB], BF16) for i in range(6)])
        rstd_ring = Ring([sb(f"rstd{i}", [128, TB], F32) for i in range(1)])
        gnr = Ring([sb(f"gnr{i}", [128, TB], F32) for i in range(1)])
        gsq = Ring([sb(f"gsq{i}", [128, TB], BF16) for i in range(3)])
        gsrc = Ring([sb(f"gsrc{i}", [128, TB], F32) for i in range(3)])
        ident_f = sb("ident_f", [128, 128], F32)
        ones_f = sb("ones_f", [128, 128], F32)
        tri_f = sb("tri_f", [128, 128], F32)
        ident_b = sb("ident_b", [128, 128], BF16)
        mean_b = sb("mean_b", [128, 128], BF16)
        blk_b = sb("blk_b", [128, 128], BF16)
        mask_b = sb("mask_b", [128, 128], BF16)
        vecs = sb("vecs_s", [128, 60], F32)
        ccol = sb("ccol_s", [128, 8], F32)
        scol = sb("scol", [128, 8], F32)
        adabc = sb("adabc_s", [128, 72], F32)
        modc = sb("modc", [128, 72], F32)
        acol = sb("acol", [128, 24], F32)
        gcol = sb("gcol", [128, 24], F32)
        Gall = sb("Gall", [128, 32, 8], F32)
        biasT = sb("biasT", [128, 32, 8], F32)
        carry = sb("carry", [128, 8], F32)
        gref = sb("gref", [128, 8], F32)
        fz = Ring([sb(f"fz{i}", [128, 8], F32) for i in range(4)])

        KTf = KT[:].rearrange("p a s -> p (a s)").bitcast(F32)
        ada_ring = Ring([KTf[:, i * 1024:(i + 1) * 1024] for i in range(8)])
        mod_ops = []
        pbanks = [psb(f"pb{i}", [128, 512], F32) for i in range(8)]
        PP = PsumPool(pbanks)

        R_xT = [Res() for _ in range(8)]
        R_hT = [Res() for _ in range(8)]
        R_yT = [Res() for _ in range(8)]
        R_act = [Res() for _ in range(12)]
        R_qT = [Res() for _ in range(8)]
        R_KT = [[Res() for _ in range(NB)] for _ in range(4)]
        R_V = [Res() for _ in range(32)]
        R_Vall = Res()
        R_u = [Res() for _ in range(4)]
        R_ws = [Res() for _ in range(NSLOT)]
        R_const = Res()
        R_cols = Res()
        R_G = Res()
        R_bias = Res()
        R_carry = Res()
        R_gref = Res()
        R_adaw = [Res() for _ in range(4)]

        ch_ws = [SC.chan() for _ in range(NSLOT)]
        ch_conv = [SC.chan() for _ in range(8)]
        ch_misc = [SC.chan() for _ in range(4)]
        ch_x = [SC.chan() for _ in range(2)]
        ch_out = [SC.chan() for _ in range(2)]
        ch_ada = [SC.chan() for _ in range(4)]

        R_sg = [[Res() for _ in range(11)] for _ in range(2)]
        R_su = [[Res() for _ in range(11)] for _ in range(2)]
        R_sd = [[Res() for _ in range(16)] for _ in range(2)]
        R_sin = [Res() for _ in range(12)]
        R_sout = [Res() for _ in range(4)]
        conv_i = [0]

        DIRECT = {}

        def conv(out_ap, in_ap, res, pat=None):
            if DIRECT_FIRST:
                DIRECT[id(res)] = (in_ap, pat)
                return
            c = ch_conv[conv_i[0] % len(ch_conv)]
            conv_i[0] += 1
            SC.dma("gpsimd", c, out_ap, in_ap, writes=[res])

        def conv_colunit(dst, src, c0, ncol, res, width):
            o = dst[:, 0:8 * ncol].rearrange("p (k f) -> p k f", k=8)
            i = src[:, c0:c0 + ncol].rearrange("(k p) f -> p k f", p=128)
            conv(o, i, res, ("p (k f) -> p k f", dict(k=8)))

        def conv_ffn(l, wg, wu, wd):
            for u in range(11):
                conv_colunit(sg[l][u], wg, u * 256, 256, R_sg[l][u], 2048)
                conv_colunit(su[l][u], wu, u * 256, 256, R_su[l][u], 2048)
            for h, (f0, nf) in enumerate(FH):
                for dc in range(8):
                    o = sd[l][h * 8 + dc][:, 0:nf * 128].rearrange("p (j d) -> p j d", j=nf)
                    i = wd[f0 * 128:(f0 + nf) * 128, dc * 128:(dc + 1) * 128].rearrange(
                        "(j p) d -> p j d", p=128)
                    conv(o, i, R_sd[l][h * 8 + dc], ("p (j d) -> p j d", dict(j=nf)))

        IN_UNITS = [(0, 256), (256, 256), (512, 256), (768, 256), (1024, 256), (1280, 264),
                    (1544, 256), (1800, 256), (2056, 256), (2312, 256), (2568, 256), (2824, 256)]

        def conv_all():
            conv_ffn(0, w1g_d, w1u_d, w1d_d)
            for u, (c0, ncol) in enumerate(IN_UNITS):
                conv_colunit(sin[u], win_d, c0, ncol, R_sin[u], 2112)
            for u in range(4):
                conv_colunit(sout[u], wout_d, u * 256, 256, R_sout[u], 2048)
            conv_ffn(1, w2g_d, w2u_d, w2d_d)

        ws_i = [0]

        def load_unit(src_ap, nelem, src_res):
            k = ws_i[0] % NSLOT
            ws_i[0] += 1
            if ALWAYS_DIRECT and id(src_res) in DIRECT:
                (in_ap, (pstr, pkw)) = DIRECT[id(src_res)]
                SC.dma("gpsimd", ch_ws[k], wsl[k][:, 0:nelem].rearrange(pstr, **pkw), in_ap, writes=[R_ws[k]])
            elif src_res.w is None and id(src_res) in DIRECT:
                (in_ap, (pstr, pkw)) = DIRECT[id(src_res)]
                SC.dma("gpsimd", ch_ws[k], wsl[k][:, 0:nelem].rearrange(pstr, **pkw), in_ap, writes=[R_ws[k]])
                c = ch_conv[conv_i[0] % len(ch_conv)]
                conv_i[0] += 1
                SC.dma("sync", c, src_ap, wsl[k][:, 0:nelem], reads=[R_ws[k]], writes=[src_res])
            else:
                SC.dma("sync", ch_ws[k], wsl[k][:, 0:nelem], src_ap, reads=[src_res], writes=[R_ws[k]])
            return wsl[k], R_ws[k]

        def setup():
            g = "gpsimd"
            SC.op(g, lambda e: e.memset(ident_f[:], 0.0), writes=[R_const])
            SC.op(g, lambda e: e.affine_select(ident_f[:], ident_f[:], [[-1, 128]], ALU.not_equal, 1.0,
                                               base=0, channel_multiplier=1), writes=[R_const])
            SC.op(g, lambda e: e.memset(ones_f[:], 1.0), writes=[R_const])
            SC.op(g, lambda e: e.memset(tri_f[:], 1.0), writes=[R_const])
            SC.op(g, lambda e: e.affine_select(tri_f[:], tri_f[:], [[1, 128]], ALU.is_ge, 0.0,
                                               base=0, channel_multiplier=-1), writes=[R_const])
            SC.op(g, lambda e: e.memset(ident_b[:], 0.0), writes=[R_const])
            SC.op(g, lambda e: e.affine_select(ident_b[:], ident_b[:], [[-1, 128]], ALU.not_equal, 1.0,
                                               base=0, channel_multiplier=1), writes=[R_const])
            SC.op(g, lambda e: e.memset(mean_b[:], 1.0 / 1024.0), writes=[R_const])
            SC.op(g, lambda e: e.memset(blk_b[:], 1.0 / 64.0), writes=[R_const])
            SC.op(g, lambda e: e.memset(blk_b[0:64, 64:128], 0.0), writes=[R_const])
            SC.op(g, lambda e: e.memset(blk_b[64:128, 0:64], 0.0), writes=[R_const])
            SC.op(g, lambda e: e.memset(mask_b[:], 0.0), writes=[R_const])
            SC.op(g, lambda e: e.affine_select(mask_b[:], mask_b[:], [[1, 128]], ALU.is_ge, NEG,
                                               base=0, channel_multiplier=-1), writes=[R_const])
            SC.op(g, lambda e: e.memset(carry[:], 0.0), writes=[R_carry])
            SC.op(g, lambda e: e.memset(ubuf[:], 0.0), writes=R_u)
            SC.op(g, lambda e: e.memset(qT[:], 0.0), writes=R_qT)
            SC.dma("sync", ch_misc[0], vecs[:], vecs_d.ap(), writes=[R_cols])
            SC.dma("sync", ch_misc[1], ccol[:], ccol_d.ap(), writes=[R_cols])

        def mod_setup():
            SC.op("scalar", lambda e: e.activation(scol[:], ccol[:], AF.Silu), reads=[R_cols], writes=[R_cols])
            SC.dma("sync", ch_misc[2], adabc[:], adabc_d.ap(), writes=[R_cols])
            SC.op("gpsimd", lambda e: e.memset(Vc[:, :, :, 64:128], 1.0), writes=[R_Vall] + R_V)

        ada_i = [0]

        def mod_vec(v):
            accs = [ftmp.get(), ftmp.get()]
            for kc in range(8):
                k = ada_i[0] % 4
                ada_i[0] += 1
                (awt, Rawt) = ada_ring.get()
                SC.dma("sync", ch_ada[k], awt,
                       adaw_d[kc * 128:(kc + 1) * 128, v * 1024:(v + 1) * 1024], writes=[Rawt])
                for hc in range(2):
                    (acc, Racc) = accs[hc]
                    if kc == 0:
                        mod_ops.append(SC.op("vector", lambda e, acc=acc, awt=awt, hc=hc, kc=kc: e.tensor_scalar(
                            acc[:], awt[:, hc * 512:(hc + 1) * 512], scol[:, kc:kc + 1], None, ALU.mult),
                            reads=[Rawt, R_cols], writes=[Racc]))
                    else:
                        mod_ops.append(SC.op("vector", lambda e, acc=acc, awt=awt, hc=hc, kc=kc: e.scalar_tensor_tensor(
                            acc[:], awt[:, hc * 512:(hc + 1) * 512], scol[:, kc:kc + 1], acc[:], ALU.mult, ALU.add),
                            reads=[Rawt, R_cols], writes=[Racc]))
            (pc, Rpc) = PP.alloc()
            for fcn in range(8):
                (acc, Racc) = accs[fcn // 4]
                SC.op("tensor", lambda e, pc=pc, acc=acc, fcn=fcn: e.matmul(
                    pc[:, fcn:fcn + 1], acc[:, (fcn % 4) * 128:(fcn % 4 + 1) * 128], ones_f[:, 0:1],
                    start=True, stop=True), reads=[Racc, R_const], writes=[Rpc])
            SC.op("vector", lambda e, pc=pc, v=v: e.tensor_tensor(
                modc[:, v * 8:(v + 1) * 8], pc[:, 0:8], adabc[:, v * 8:(v + 1) * 8], ALU.add),
                reads=[Rpc, R_cols], writes=[R_cols])
            PP.release((pc, Rpc))

        def mod_derive(l):
            sc_ = modc[:, (3 * l + 1) * 8:(3 * l + 2) * 8]
            ng = vecs[:, l * 8:(l + 1) * 8]
            SC.op("vector", lambda e, sc_=sc_, ng=ng, l=l: e.scalar_tensor_tensor(
                acol[:, l * 8:(l + 1) * 8], sc_, 1.0, ng, ALU.add, ALU.mult),
                reads=[R_cols], writes=[R_cols])

        def mod_gate(l):
            g_ = modc[:, (3 * l + 2) * 8:(3 * l + 3) * 8]
            fac = 1.0 if l == 1 else 0.5
            SC.op("vector", lambda e, g_=g_, l=l, fac=fac: e.tensor_scalar(
                gcol[:, l * 8:(l + 1) * 8], g_, fac, None, ALU.mult),
                reads=[R_cols], writes=[R_cols])

        def shcol(l, dc):
            return modc[:, 3 * l * 8 + dc:3 * l * 8 + dc + 1]

        xq = []
        actTf = actT[:].rearrange("p j t -> p (j t)").bitcast(F32)
        xtiles = [(stage.tiles[0][0][:], [stage.tiles[0][1]])]
        for k_ in range(3):
            xtiles.append((actTf[:, k_ * 1024:(k_ + 1) * 1024], R_act[4 * k_:4 * k_ + 4]))

        def prefetch_x(i, tts):
            for tt in tts:
                (stg, Rsl) = xtiles[tt]
                r0 = (4 * i + tt) * 128
                SC.dma("sync", ch_x[tt % 2], stg, x_d[r0:r0 + 128, :], writes=Rsl)
                xq.append((stg, Rsl))

        def load_block(i):
            for tt in range(4):
                (stg, Rsl) = xq.pop(0)
                for half in range(2):
                    (pt, Rpt) = PP.alloc()
                    for q in range(4):
                        dc = half * 4 + q
                        SC.op("tensor", lambda e, pt=pt, q=q, stg=stg, dc=dc: e.transpose(
                            pt[:, q * 128:(q + 1) * 128], stg[:, dc * 128:(dc + 1) * 128], ident_f[:]),
                            reads=list(Rsl) + [R_const], writes=[Rpt])
                    eng = "scalar" if half == 0 else "vector"
                    o = xT[:, half * 4:half * 4 + 4, tt * 128:(tt + 1) * 128]
                    src = pt[:, :].rearrange("p (q t) -> p q t", q=4)
                    if eng == "scalar":
                        SC.op(eng, lambda e, o=o, src=src: e.copy(o, src), reads=[Rpt],
                              writes=R_xT[half * 4:half * 4 + 4])
                    else:
                        SC.op(eng, lambda e, o=o, src=src: e.tensor_copy(o, src), reads=[Rpt],
                              writes=R_xT[half * 4:half * 4 + 4])
                    PP.release((pt, Rpt))

        def norm_stats():
            (ps, Rps) = PP.alloc()
            for dc in range(8):
                (sq, Rsq) = btmp.get()
                SC.op("scalar", lambda e, sq=sq, dc=dc: e.activation(sq[:], xT[:, dc, :], AF.Square),
                      reads=[R_xT[dc]], writes=[Rsq])
                SC.op("tensor", lambda e, ps=ps, sq=sq, dc=dc: e.matmul(
                    ps[:], mean_b[:], sq[:], start=(dc == 0), stop=(dc == 7)),
                    reads=[Rsq, R_const], writes=[Rps])
            (rs, Rrs) = rstd_ring.get()
            SC.op("scalar", lambda e, rs=rs, ps=ps: e.activation(rs[:], ps[:], AF.Ln, bias=EPS, scale=1.0),
                  reads=[Rps], writes=[Rrs])
            PP.release((ps, Rps))
            SC.op("scalar", lambda e, rs=rs: e.activation(rs[:], rs[:], AF.Exp, scale=-0.5),
                  reads=[], writes=[Rrs])
            return rs, Rrs

        def norm_mod(l):
            rs, Rrs = norm_stats()
            for dc in range(8):
                (t, Rt) = ftmp.get()
                SC.op("vector", lambda e, t=t, dc=dc, rs=rs, l=l: e.scalar_tensor_tensor(
                    t[:], xT[:, dc, :], acol[:, l * 8 + dc:l * 8 + dc + 1], rs[:], ALU.mult, ALU.mult),
                    reads=[R_xT[dc], Rrs, R_cols], writes=[Rt])
                SC.op("scalar", lambda e, t=t, dc=dc, l=l: e.activation(
                    hT[:, dc, :], t[:], AF.Identity, bias=shcol(l, dc), scale=1.0),
                    reads=[Rt, R_cols], writes=[R_hT[dc]])

        def ffn(l, gl, hook=None):
            for h, (f0, nf) in enumerate(FH):
                units = {}

                def unit(u):
                    if u not in units:
                        gs, Rgs = load_unit(sg[l][u], 2048, R_sg[l][u])
                        us, Rus = load_unit(su[l][u], 2048, R_su[l][u])
                        units[u] = (gs[:, 0:2048].rearrange("p (k f) -> p k f", k=8), Rgs,
                                    us[:, 0:2048].rearrange("p (k f) -> p k f", k=8), Rus)
                    return units[u]

                groups = [[f0, f0 + 1, f0 + 2]] + [[fc] for fc in range(f0 + 3, f0 + nf)] if h == 0 \
                    else [[fc] for fc in range(f0, f0 + nf)]
                for grp in groups:
                    info = []
                    for fc in grp:
                        gv, Rgs, uv, Rus = unit(fc // 2)
                        info.append((fc, gv, Rgs, uv, Rus, PP.alloc(), PP.alloc()))
                    if len(grp) == 1:
                        order = [(it, w, kc) for it in info for w in (0, 1) for kc in range(8)]
                    else:
                        order = [(it, w, kc) for kc in range(8) for it in info for w in (0, 1)]
                    for (it, w, kc) in order:
                        (fc, gv, Rgs, uv, Rus, (pg, Rpg), (pu, Rpu)) = it
                        s0 = (fc % 2) * 128
                        if w == 0:
                            SC.op("tensor", lambda e, pg=pg, gv=gv, kc=kc, s0=s0: e.matmul(
                                pg[:], gv[:, kc, s0:s0 + 128], hT[:, kc, :], start=(kc == 0), stop=(kc == 7)),
                                reads=[Rgs, R_hT[kc]], writes=[Rpg])
                        else:
                            SC.op("tensor", lambda e, pu=pu, uv=uv, kc=kc, s0=s0: e.matmul(
                                pu[:], uv[:, kc, s0:s0 + 128], hT[:, kc, :], start=(kc == 0), stop=(kc == 7)),
                                reads=[Rus, R_hT[kc]], writes=[Rpu])
                    for it in info:
                        (fc, gv, Rgs, uv, Rus, (pg, Rpg), (pu, Rpu)) = it
                        j = fc - f0
                        (sl, Rsl) = btmp.get()
                        SC.op("scalar", lambda e, sl=sl, pg=pg: e.activation(sl[:], pg[:], AF.Silu),
                              reads=[Rpg], writes=[Rsl])
                        PP.release((pg, Rpg))
                        SC.op("vector", lambda e, j=j, pu=pu, sl=sl: e.tensor_tensor(
                            actT[:, j, :], pu[:], sl[:], ALU.mult), reads=[Rpu, Rsl], writes=[R_act[j]])
                        PP.release((pu, Rpu))
                if hook is not None:
                    hook(h)
                for dc in range(8):
                    ds, Rds = load_unit(sd[l][h * 8 + dc][:, 0:nf * 128], nf * 128, R_sd[l][h * 8 + dc])
                    dv = ds[:, 0:nf * 128].rearrange("p (j d) -> p j d", j=nf)
                    (py, Rpy) = PP.alloc()
                    for j in range(nf):
                        SC.op("tensor", lambda e, py=py, dv=dv, j=j, nf=nf: e.matmul(
                            py[:], dv[:, j, :], actT[:, j, :], start=(j == 0), stop=(j == nf - 1)),
                            reads=[Rds, R_act[j]], writes=[Rpy])
                    SC.op("vector", lambda e, py=py, dc=dc, gl=gl: e.scalar_tensor_tensor(
                        xT[:, dc, :], py[:], gcol[:, gl * 8 + dc:gl * 8 + dc + 1], xT[:, dc, :],
                        ALU.mult, ALU.add), reads=[Rpy, R_cols], writes=[R_xT[dc]])
                    PP.release((py, Rpy))

        gn_pending = []

        def groupnorm_to_y(src, Rsrc, kc):
            gn_pending.append((src, Rsrc, kc))

        def gn_flush(keep=0):
            while len(gn_pending) > keep:
                (src, Rsrc, kc) = gn_pending.pop(0)
                (sq, Rsq) = gsq.get()
                SC.op("scalar", lambda e, sq=sq, src=src: e.activation(sq[:], src[:], AF.Square),
                      reads=[Rsrc], writes=[Rsq])
                (ps, Rps) = PP.alloc()
                SC.op("tensor", lambda e, ps=ps, sq=sq: e.matmul(ps[:], blk_b[:], sq[:], start=True, stop=True),
                      reads=[Rsq, R_const], writes=[Rps])
                (rs, Rrs) = gnr.get()
                SC.op("scalar", lambda e, rs=rs, ps=ps: e.activation(rs[:], ps[:], AF.Ln, bias=EPS, scale=1.0),
                      reads=[Rps], writes=[Rrs])
                PP.release((ps, Rps))
                SC.op("scalar", lambda e, rs=rs: e.activation(rs[:], rs[:], AF.Exp, scale=-0.5),
                      reads=[], writes=[Rrs])
                SC.op("vector", lambda e, src=src, rs=rs, kc=kc: e.scalar_tensor_tensor(
                    yT[:, kc, :], src[:], vecs[:, 32 + kc:33 + kc], rs[:], ALU.mult, ALU.mult),
                    reads=[Rsrc, Rrs, R_cols], writes=[R_yT[kc]])

        def in_proj(i):
            SC.op("vector", lambda e: e.tensor_copy(gref[:], carry[:]), reads=[R_carry], writes=[R_gref])
            for up in range(2):
                cs, Rc = load_unit(sin[8 + up][:, 0:2048], 2048, R_sin[8 + up])
                xs, Rx = load_unit(sin[10 + up][:, 0:2048], 2048, R_sin[10 + up])
                bs, Rb = load_unit(sin[6 + up][:, 0:2048], 2048, R_sin[6 + up])
                bv = bs[:, 0:2048].rearrange("p (k f) -> p k f", k=8)
                cv_ = cs[:, 0:2048].rearrange("p (k f) -> p k f", k=8)
                xv = xs[:, 0:2048].rearrange("p (k f) -> p k f", k=8)
                for s_ in range(2):
                    cc = up * 2 + s_
                    outs = [PP.alloc() for _ in range(3)]
                    for kc in range(8):
                        for (wv, Rw), (pp, Rpp) in zip(((cv_, Rc), (xv, Rx), (bv, Rb)), outs):
                            SC.op("tensor", lambda e, pp=pp, wv=wv, kc=kc, s_=s_: e.matmul(
                                pp[:], wv[:, kc, s_ * 128:(s_ + 1) * 128], hT[:, kc, :],
                                start=(kc == 0), stop=(kc == 7)), reads=[Rw, R_hT[kc]], writes=[Rpp])
                    (pC, RpC), (pX, RpX), (pB, RpB) = outs
                    gn_flush(keep=1)
                    (csb, Rcsb) = ftmp.get()
                    SC.op("scalar", lambda e, csb=csb, pC=pC: e.copy(csb[:], pC[:]), reads=[RpC], writes=[Rcsb])
                    PP.release((pC, RpC))
                    SC.op("vector", lambda e, cc=cc, pX=pX, csb=csb: e.tensor_tensor(
                        ubuf[:, cc, 2:TB + 2], pX[:], csb[:], ALU.mult), reads=[RpX, Rcsb], writes=[R_u[cc]])
                    PP.release((pX, RpX))
                    (bsb, Rbsb) = gsrc.get()
                    SC.op("scalar", lambda e, bsb=bsb, pB=pB: e.copy(bsb[:], pB[:]), reads=[RpB], writes=[Rbsb])
                    PP.release((pB, RpB))
                    w0 = vecs[:, 40 + cc:41 + cc]
                    w1 = vecs[:, 44 + cc:45 + cc]
                    w2 = vecs[:, 48 + cc:49 + cc]
                    (y1, Ry1) = ftmp.get()
                    SC.op("scalar", lambda e, y1=y1, cc=cc, w2=w2: e.activation(
                        y1[:], ubuf[:, cc, 2:TB + 2], AF.Identity, scale=w2), reads=[R_u[cc], R_cols], writes=[Ry1])
                    SC.op("vector", lambda e, y1=y1, cc=cc, w1=w1: e.scalar_tensor_tensor(
                        y1[:], ubuf[:, cc, 1:TB + 1], w1, y1[:], ALU.mult, ALU.add),
                        reads=[R_u[cc], R_cols], writes=[Ry1])
                    SC.op("vector", lambda e, y1=y1, cc=cc, w0=w0: e.scalar_tensor_tensor(
                        y1[:], ubuf[:, cc, 0:TB], w0, y1[:], ALU.mult, ALU.add),
                        reads=[R_u[cc], R_cols], writes=[Ry1])
                    SC.op("vector", lambda e, cc=cc: e.tensor_copy(ubuf[:, cc, 0:2], ubuf[:, cc, TB:TB + 2]),
                          reads=[], writes=[R_u[cc]])
                    SC.op("vector", lambda e, bsb=bsb, y1=y1: e.tensor_tensor(
                        bsb[:], bsb[:], y1[:], ALU.mult), reads=[Ry1], writes=[Rbsb])
                    groupnorm_to_y(bsb, Rbsb, 4 + cc)
            for u in range(4):
                ws, Rw = load_unit(sin[u][:, 0:2048], 2048, R_sin[u])
                wv = ws[:, 0:2048].rearrange("p (k f) -> p k f", k=8)
                for s_ in range(2):
                    j = (u % 2) * 2 + s_
                    (pq, Rpq) = PP.alloc()
                    for kc in range(8):
                        SC.op("tensor", lambda e, pq=pq, wv=wv, kc=kc, s_=s_: e.matmul(
                            pq[:], wv[:, kc, s_ * 128:(s_ + 1) * 128], hT[:, kc, :],
                            start=(kc == 0), stop=(kc == 7)), reads=[Rw, R_hT[kc]], writes=[Rpq])
                    if u < 2:
                        SC.op("scalar", lambda e, pq=pq, j=j: e.copy(qT[0:64, 2 * j, :], pq[0:64, :]),
                              reads=[Rpq], writes=[R_qT[2 * j]])
                        SC.op("scalar", lambda e, pq=pq, j=j: e.copy(qT[64:128, 2 * j + 1, :], pq[64:128, :]),
                              reads=[Rpq], writes=[R_qT[2 * j + 1]])
                    else:
                        SC.op("vector", lambda e, pq=pq, j=j, i=i: e.tensor_copy(
                            KT[:, j, i * TB:(i + 1) * TB], pq[:]), reads=[Rpq], writes=[R_KT[j][i]],
                            extra=(mod_ops[-1:] if i == 0 else ()))
                    PP.release((pq, Rpq))
            gn_flush()
            v0, Rv0 = load_unit(sin[4][:, 0:2048], 2048, R_sin[4])
            v1, Rv1 = load_unit(sin[5][:, 0:2112], 2112, R_sin[5])
            v0v = v0[:, 0:2048].rearrange("p (k f) -> p k f", k=8)
            v1v = v1[:, 0:2112].rearrange("p (k f) -> p k f", k=8)
            for tt in range(4):
                kb = 4 * i + tt
                (pv0, Rpv0) = PP.alloc()
                (pv1, Rpv1) = PP.alloc()
                for kc in range(8):
                    SC.op("tensor", lambda e, pv0=pv0, kc=kc, tt=tt: e.matmul(
                        pv0[:, 0:256], hT[:, kc, tt * 128:(tt + 1) * 128], v0v[:, kc, :],
                        start=(kc == 0), stop=(kc == 7)), reads=[Rv0, R_hT[kc]], writes=[Rpv0])
                for kc in range(8):
                    SC.op("tensor", lambda e, pv1=pv1, kc=kc, tt=tt: e.matmul(
                        pv1[:, 0:264], hT[:, kc, tt * 128:(tt + 1) * 128], v1v[:, kc, :],
                        start=(kc == 0), stop=(kc == 7)), reads=[Rv1, R_hT[kc]], writes=[Rpv1])
                (z, Rz) = fz.get()
                SC.op("vector", lambda e, z=z, pv1=pv1: e.tensor_tensor(
                    z[:], pv1[:, 256:264], vecs[:, 52:60], ALU.add), reads=[Rpv1, R_cols], writes=[Rz])
                for g_, (pvx, Rpvx) in enumerate(((pv0, Rpv0), (pv1, Rpv1))):
                    src = pvx[:, 0:256].rearrange("p (a b d) -> p a b d", a=2, b=2)
                    dst = Vc[:, kb, 2 * g_:2 * g_ + 2, :].rearrange("p a (b d) -> p a b d", b=3)[:, :, 0:3:2, :]
                    if g_ == 0:
                        SC.op("scalar", lambda e, dst=dst, src=src: e.copy(dst, src),
                              reads=[Rpvx], writes=[R_V[kb]])
                    else:
                        SC.op("vector", lambda e, dst=dst, src=src: e.tensor_copy(dst, src),
                              reads=[Rpvx], writes=[R_V[kb]])
                PP.release((pv0, Rpv0))
                PP.release((pv1, Rpv1))
                (ez, Rez) = fz.get()
                SC.op("scalar", lambda e, ez=ez, z=z: e.activation(ez[:], z[:], AF.Exp, scale=-1.0),
                      reads=[Rz], writes=[Rez])
                (sp, Rsp) = fz.get()
                SC.op("scalar", lambda e, sp=sp, ez=ez: e.activation(sp[:], ez[:], AF.Ln, bias=1.0, scale=1.0),
                      reads=[Rez], writes=[Rsp])
                (pf, Rpf) = PP.alloc()
                SC.op("tensor", lambda e, pf=pf, sp=sp: e.matmul(pf[:, 0:8], tri_f[:], sp[:], start=True, stop=True),
                      reads=[Rsp, R_const], writes=[Rpf])
                SC.op("tensor", lambda e, pf=pf, sp=sp: e.matmul(pf[:, 8:16], ones_f[:], sp[:], start=True, stop=True),
                      reads=[Rsp, R_const], writes=[Rpf])
                SC.op("vector", lambda e, pf=pf, kb=kb: e.tensor_tensor(
                    Gall[:, kb, :], pf[:, 0:8], carry[:], ALU.add), reads=[Rpf, R_carry], writes=[R_G])
                SC.op("vector", lambda e, pf=pf: e.tensor_tensor(
                    carry[:], pf[:, 8:16], carry[:], ALU.add), reads=[Rpf], writes=[R_carry])
                PP.release((pf, Rpf))

        LA = 4
        FASTRECIP = False

        def attention(i):
            nkb = 4 * i + 4
            SC.op("vector", lambda e: e.tensor_tensor(gref[:], gref[:], carry[:], ALU.add),
                  reads=[R_carry], writes=[R_gref])
            SC.op("vector", lambda e: e.tensor_scalar(gref[:], gref[:], 0.5, None, ALU.mult),
                  reads=[], writes=[R_gref])
            for h in range(8):
                SC.op("vector", lambda e, h=h, nkb=nkb: e.tensor_scalar(
                    biasT[:, 0:nkb, h], Gall[:, 0:nkb, h], gref[:, h:h + 1], None, ALU.subtract),
                    reads=[R_G, R_gref], writes=[R_bias])
            steps = [(h, kb) for h in range(8) for kb in range(nkb)]
            sbank = {}
            fin_pending = []

            def emit_qk(si):
                h, kb = steps[si]
                pr, half = h // 2, h % 2
                b0 = 64 * half
                dq = kb - 4 * i
                n0 = max(0, dq) * 128
                (ps, Rps) = PP.alloc()
                sbank[si] = (ps, Rps)
                SC.op("tensor", lambda e, ps=ps, pr=pr, kb=kb, h=h, n0=n0, dq=dq: e.matmul(
                    ps[:, n0:TB], KT[:, pr, kb * 128:(kb + 1) * 128], qT[:, h, n0:TB],
                    start=True, stop=(dq < 0)),
                    reads=[R_KT[pr][kb // 4], R_qT[h]], writes=[Rps])
                if dq >= 0:
                    SC.op("tensor", lambda e, ps=ps, n0=n0: e.matmul(
                        ps[:, n0:n0 + 128], ident_b[:], mask_b[:], start=False, stop=True),
                        reads=[R_const], writes=[Rps])

            for si in range(min(LA, len(steps))):
                emit_qk(si)
            cur = {}
            for si, (h, kb) in enumerate(steps):
                pr, half = h // 2, h % 2
                b0 = 64 * half
                dq = kb - 4 * i
                n0 = max(0, dq) * 128
                if si + LA < len(steps):
                    emit_qk(si + LA)
                if kb == 0:
                    cur["po"] = PP.alloc()
                    if half == 0:
                        cur["onp"] = gsrc.get()
                if kb == min(10, nkb - 1) and half == 0:
                    gn_flush()
                (po, Rpo) = cur["po"]
                (onp, Ronp) = cur["onp"]
                (ps, Rps) = sbank.pop(si)
                (pt, Rpt) = btmp.get()
                SC.op("scalar", lambda e, pt=pt, ps=ps, n0=n0, kb=kb, h=h: e.activation(
                    pt[:, n0:TB], ps[:, n0:TB], AF.Exp, bias=biasT[:, kb, h:h + 1], scale=0.125),
                    reads=[Rps, R_bias], writes=[Rpt])
                PP.release((ps, Rps))
                off = 64 * half
                SC.op("tensor", lambda e, po=po, pt=pt, kb=kb, pr=pr, off=off, n0=n0, nkb=nkb: e.matmul(
                    po[:, n0:TB], Vc[:, kb, pr, off:off + 128], pt[:, n0:TB],
                    start=(kb == 0), stop=(kb == nkb - 1)),
                    reads=[R_V[kb], Rpt], writes=[Rpo])
                while fin_pending and fin_pending[0][0] <= si:
                    fin_pending.pop(0)[1]()
                if kb == nkb - 1:
                    def finalize(po=po, Rpo=Rpo, onp=onp, Ronp=Ronp, b0=b0, half=half, pr=pr):
                        d0 = 64 - b0
                        (osb, Rosb) = ftmp.get()
                        SC.op("scalar", lambda e: e.copy(osb[:], po[:]), reads=[Rpo], writes=[Rosb])
                        PP.release((po, Rpo))
                        (rc, Rrc) = ftmp.get()
                        if FASTRECIP:
                            SC.op("vector", lambda e: e.reciprocal_approx_fast(
                                rc[b0:b0 + 64, :], osb[d0:d0 + 64, :]), reads=[Rosb], writes=[Rrc])
                        else:
                            SC.op("vector", lambda e: e.reciprocal(
                                rc[b0:b0 + 64, :], osb[d0:d0 + 64, :]), reads=[Rosb], writes=[Rrc])
                        SC.op("vector", lambda e: e.tensor_tensor(
                            onp[b0:b0 + 64, :], osb[b0:b0 + 64, :], rc[b0:b0 + 64, :], ALU.mult),
                            reads=[Rosb, Rrc], writes=[Ronp])
                        if half == 1:
                            groupnorm_to_y(onp, Ronp, pr)
                    fin_pending.append((si + 2, finalize))
            while fin_pending:
                fin_pending.pop(0)[1]()

        def out_proj():
            units = {}

            def unit(u):
                if u not in units:
                    ws, Rw = load_unit(sout[u], 2048, R_sout[u])
                    units[u] = (ws[:, 0:2048].rearrange("p (k f) -> p k f", k=8), Rw)
                return units[u]

            def mm(py, Rpy, dc, kcs, first, last):
                wv, Rw = unit(dc // 2)
                s_ = dc % 2
                for n_, kc in enumerate(kcs):
                    SC.op("tensor", lambda e, py=py, wv=wv, kc=kc, s_=s_, n_=n_: e.matmul(
                        py[:], wv[:, kc, s_ * 128:(s_ + 1) * 128], yT[:, kc, :],
                        start=(first and n_ == 0), stop=(last and n_ == len(kcs) - 1)),
                        reads=[Rw, R_yT[kc]], writes=[Rpy])

            def evac(py, Rpy, dc):
                SC.op("vector", lambda e, py=py, dc=dc: e.scalar_tensor_tensor(
                    xT[:, dc, :], py[:], gcol[:, 8 + dc:9 + dc], xT[:, dc, :], ALU.mult, ALU.add),
                    reads=[Rpy, R_cols], writes=[R_xT[dc]])
                PP.release((py, Rpy))

            held = []
            for dc in range(6):
                (py, Rpy) = PP.alloc()
                mm(py, Rpy, dc, (4, 5, 6, 7), True, False)
                held.append((py, Rpy, dc))
            gn_flush()
            for (py, Rpy, dc) in held:
                mm(py, Rpy, dc, (0, 1, 2, 3), False, True)
                evac(py, Rpy, dc)
            for dc in (6, 7):
                (py, Rpy) = PP.alloc()
                mm(py, Rpy, dc, (4, 5, 6, 7, 0, 1, 2, 3), True, True)
                evac(py, Rpy, dc)

        store_ops = []

        def final_store(i):
            if fnorm:
                rs, Rrs = norm_stats()
                for dc in range(8):
                    SC.op("vector", lambda e, dc=dc, rs=rs: e.scalar_tensor_tensor(
                        xT[:, dc, :], xT[:, dc, :], vecs[:, 24 + dc:25 + dc], rs[:], ALU.mult, ALU.mult),
                        reads=[Rrs, R_cols], writes=[R_xT[dc]])
            for tt in range(4):
                r0 = (4 * i + tt) * 128
                for half in range(2):
                    (stg, Rs) = ostage.get()
                    (pt, Rpt) = PP.alloc()
                    for q in range(4):
                        dc = half * 4 + q
                        SC.op("tensor", lambda e, pt=pt, q=q, dc=dc, tt=tt: e.transpose(
                            pt[:, q * 128:(q + 1) * 128], xT[:, dc, tt * 128:(tt + 1) * 128], ident_f[:]),
                            reads=[R_xT[dc], R_const], writes=[Rpt])
                    if half == 0:
                        SC.op("scalar", lambda e, stg=stg, pt=pt: e.copy(stg[:], pt[:]),
                              reads=[Rpt], writes=[Rs])
                    else:
                        SC.op("vector", lambda e, stg=stg, pt=pt: e.tensor_copy(stg[:], pt[:]),
                              reads=[Rpt], writes=[Rs])
                    PP.release((pt, Rpt))
                    o = SC.dma("gpsimd", ch_out[half], out_d[r0:r0 + 128, half * 512:(half + 1) * 512], stg[:],
                               reads=[Rs])
                    store_ops.append(o)

        setup()
        mod_setup()
        mod_vec(0)
        mod_vec(1)
        mod_derive(0)
        conv_all()
        prefetch_x(0, [0, 1, 2, 3])
        for i in range(nblocks):
            load_block(i)
            if upto >= 1:
                norm_mod(0)
                if dbg and i == 0:
                    chd = SC.chan()
                    store_ops.append(SC.dma("gpsimd", chd, dbg_d[:, 0:72], modc[:], reads=[R_cols]))
                    store_ops.append(SC.dma("gpsimd", chd, dbg_d[:, 72:96], acol[:], reads=[R_cols]))
                    store_ops.append(SC.dma("gpsimd", chd, dbg_d[:, 96:120], gcol[:], reads=[R_cols]))
                    store_ops.append(SC.dma("gpsimd", chd, dbgh_d.ap().rearrange("p (k t) -> p k t", k=8), hT[:], reads=R_hT))
                if i == 0:
                    def hook(h):
                        if h == 0:
                            mod_vec(2)
                            mod_gate(0)
                            mod_vec(3)
                            mod_vec(4)
                            mod_derive(1)
                        else:
                            mod_vec(5)
                            mod_gate(1)
                            mod_vec(6)
                            mod_vec(7)
                            mod_derive(2)
                            mod_vec(8)
                            mod_gate(2)
                    ffn(0, 0, hook)
                else:
                    ffn(0, 0)
            if upto >= 2:
                norm_mod(1)
                in_proj(i)
                attention(i)
                out_proj()
            if i + 1 < nblocks:
                prefetch_x(i + 1, [0])
            if upto >= 3:
                norm_mod(2)
                ffn(1, 2)
            if i + 1 < nblocks:
                prefetch_x(i + 1, [1, 2, 3])
            final_store(i)
        SC.op("gpsimd", None, extra=store_ops[-12:])
        SC.emit()
    return nc


_CACHE = {}


def _col(v):
    return np.ascontiguousarray(np.asarray(v, np.float32).reshape(-1, 128).T)


def kernel(x, c, ada_w, ada_b, norm1_g, ffn1_w_gate, ffn1_w_up, ffn1_w_down,
           norm2_g, w_in, forget_bias, conv_w, group_norm_g, w_out,
           norm3_g, ffn2_w_gate, ffn2_w_up, ffn2_w_down, final_g):
    f32 = lambda a: np.ascontiguousarray(np.asarray(a, dtype=np.float32))
    x = f32(x)
    c = f32(c)
    vecs = np.zeros((128, 60), np.float32)
    vecs[:, 0:8] = _col(norm1_g)
    vecs[:, 8:16] = _col(norm2_g)
    vecs[:, 16:24] = _col(norm3_g)
    vecs[:, 24:32] = _col(final_g)
    vecs[:, 32:40] = _col(group_norm_g)
    cw = f32(conv_w)
    for j in range(3):
        vecs[:, 40 + 4 * j:44 + 4 * j] = _col(cw[j])
    vecs[:, 52:60] = np.broadcast_to(f32(forget_bias)[None, :], (128, 8))
    shared = {
        "ada_w": f32(ada_w), "adabc": _col(ada_b), "vecs": vecs,
        "w1g": f32(ffn1_w_gate), "w1u": f32(ffn1_w_up), "w1d": f32(ffn1_w_down),
        "w_in": f32(w_in), "w_out": f32(w_out),
        "w2g": f32(ffn2_w_gate), "w2u": f32(ffn2_w_up), "w2d": f32(ffn2_w_down),
    }
    in_maps = []
    for b in range(8):
        m = dict(shared)
        m["x"] = x[b]
        m["ccol"] = _col(c[b])
        in_maps.append(m)
    if "nc" not in _CACHE:
        _CACHE["nc"] = build_program()
    nc = _CACHE["nc"]
    res = run_bass_kernel_spmd(nc, in_maps, core_ids=list(range(8)))
    return np.stack([np.asarray(r["out"], dtype=np.float32) for r in res.results], axis=0)
```

```python
from contextlib import ExitStack
import numpy as np
import concourse.bass as bass
import concourse.mybir as mybir
from concourse.bass_utils import run_bass_kernel_spmd

F32 = mybir.dt.float32
BF16 = mybir.dt.bfloat16
AF = mybir.ActivationFunctionType
ALU = mybir.AluOpType

D = 1024
S = 4096
DFF = 2816
NFC = 22
INC = 3080
NB = 8
TB = 512
EPS = 1e-6
NEG = -30000.0
FH = ((0, 12), (12, 10))

ENG_NAMES = ("tensor", "vector", "scalar", "gpsimd", "sync")
SEM_ROLL = 30000


class Op:
    __slots__ = ("eng", "fn", "deps", "signal", "sem", "count", "is_dma")

    def __init__(self, eng, fn, deps, is_dma=False):
        self.eng = eng
        self.fn = fn
        self.deps = deps
        self.signal = False
        self.sem = None
        self.count = 0
        self.is_dma = is_dma


class Chan:
    def __init__(self, sem):
        self.sem = sem
        self.count = 0
        self.last = None


class Res:
    __slots__ = ("w", "r", "rd", "excl")

    def __init__(self, excl=False):
        self.w = None
        self.r = {}
        self.rd = []
        self.excl = excl


class Sched:
    def __init__(self, nc, stack):
        self.nc = nc
        self.stack = stack
        self.ops = {e: [] for e in ENG_NAMES}
        self.nsem = 0

    def new_sem(self, name=None):
        self.nsem += 1
        return self.stack.enter_context(self.nc.semaphore(name or f"s{self.nsem}"))

    def chan(self):
        return Chan(self.new_sem())

    def _deps(self, eng, reads, writes, extra, is_dma):
        deps = []
        wr = list(writes)
        for R in reads:
            if R.excl:
                wr.append(R)
            elif R.w is not None:
                deps.append(R.w)
        for R in wr:
            if R.w is not None:
                deps.append(R.w)
            deps.extend(R.r.values())
            deps.extend(R.rd)
        deps.extend(d for d in extra if d is not None)
        out = []
        seen = set()
        for d in deps:
            if id(d) in seen:
                continue
            seen.add(id(d))
            if (not is_dma) and (not d.is_dma) and d.eng == "tensor" and eng == "tensor":
                continue
            out.append(d)
        return out

    def _update(self, o, reads, writes):
        for R in reads:
            if R.excl:
                R.w = o
                R.r = {}
                R.rd = []
            elif o.is_dma:
                R.rd.append(o)
            else:
                R.r[o.eng] = o
        for R in writes:
            R.w = o
            R.r = {}
            R.rd = []

    def op(self, eng, fn, reads=(), writes=(), extra=()):
        deps = self._deps(eng, reads, writes, extra, False)
        o = Op(eng, fn, deps)
        for d in deps:
            d.signal = True
        self.ops[eng].append(o)
        self._update(o, reads, writes)
        return o

    def dma(self, eng, chan, out, in_, reads=(), writes=(), extra=()):
        deps = self._deps(eng, reads, writes, list(extra) + [chan.last], True)

        def fn(e):
            return e.dma_start(out=out, in_=in_)

        o = Op(eng, fn, deps, is_dma=True)
        for d in deps:
            d.signal = True
        chan.count += 16
        chan.last = o
        o.sem = chan.sem
        o.count = chan.count
        self.ops[eng].append(o)
        self._update(o, reads, writes)
        return o

    def emit(self):
        nc = self.nc
        for e in ENG_NAMES:
            sem = None
            cnt = 0
            for o in self.ops[e]:
                if o.is_dma:
                    continue
                if o.signal:
                    if o.fn is None:
                        raise RuntimeError("wait-only op cannot signal")
                    if sem is None or cnt >= SEM_ROLL:
                        sem = self.new_sem()
                        cnt = 0
                    cnt += 1
                    o.sem = sem
                    o.count = cnt
        with nc.Block() as block:
            for e in ENG_NAMES:
                ops = self.ops[e]
                if not ops:
                    continue

                def body(engh, ops=ops):
                    waited = {}
                    for o in ops:
                        for d in o.deps:
                            key = id(d.sem)
                            if waited.get(key, 0) >= d.count:
                                continue
                            engh.wait_ge(d.sem, d.count)
                            waited[key] = d.count
                        if o.fn is None:
                            continue
                        ins = o.fn(engh)
                        if o.is_dma:
                            ins.then_inc(o.sem, 16)
                        elif o.signal:
                            ins.then_inc(o.sem, 1)

                getattr(block, e)(body)


class Ring:
    def __init__(self, tiles):
        self.tiles = [(t, Res()) for t in tiles]
        self.i = 0

    def get(self):
        t = self.tiles[self.i % len(self.tiles)]
        self.i += 1
        return t


class PsumPool:
    def __init__(self, banks):
        self.free = [(b, Res(excl=True)) for b in banks]

    def alloc(self):
        if not self.free:
            raise RuntimeError("psum pool exhausted")
        return self.free.pop(0)

    def release(self, b):
        self.free.append(b)


def build_program(nblocks=NB, upto=99, fnorm=True, dbg=False):
    nc = bass.Bass("TRN2", target_bir_lowering=False)

    def din(name, shape, dt=F32):
        return nc.dram_tensor(name, shape, dt, kind="ExternalInput")

    x_d = din("x", [S, D])
    ccol_d = din("ccol", [128, 8])
    adaw_d = din("ada_w", [D, 9 * D])
    vecs_d = din("vecs", [128, 60])
    adabc_d = din("adabc", [128, 72])
    w1g_d = din("w1g", [D, DFF])
    w1u_d = din("w1u", [D, DFF])
    w1d_d = din("w1d", [DFF, D])
    win_d = din("w_in", [D, INC])
    wout_d = din("w_out", [D, D])
    w2g_d = din("w2g", [D, DFF])
    w2u_d = din("w2u", [D, DFF])
    w2d_d = din("w2d", [DFF, D])
    out_d = nc.dram_tensor("out", [S, D], F32, kind="ExternalOutput")
    if dbg:
        dbg_d = nc.dram_tensor("dbg", [128, 128], F32, kind="ExternalOutput")
        dbgh_d = nc.dram_tensor("dbgh", [128, 4096], F32, kind="ExternalOutput")

    def dscr(name, shape):
        return nc.dram_tensor(name, shape, BF16, kind="Internal")

    sg = [dscr("sg1", [11, 128, 2048]), dscr("sg2", [11, 128, 2048])]
    su = [dscr("su1", [11, 128, 2048]), dscr("su2", [11, 128, 2048])]
    sd = [dscr("sd1", [16, 128, 1536]), dscr("sd2", [16, 128, 1536])]
    sin = dscr("sin", [12, 128, 2112])
    sout = dscr("sout", [4, 128, 2048])

    with ExitStack() as st:
        SC = Sched(nc, st)

        def sb(name, shape, dt):
            return st.enter_context(nc.sbuf_tensor(name, shape, dt))

        def psb(name, shape, dt):
            return st.enter_context(nc.psum_tensor(name, shape, dt))

        KT = sb("KT", [128, 4, S], BF16)
        Vc = sb("Vc", [128, 32, 4, 192], BF16)
        xT = sb("xT", [128, 8, TB], F32)
        hT = sb("hT", [128, 8, TB], BF16)
        yT = sb("yT", [128, 8, TB], BF16)
        actT = sb("actT", [128, 12, TB], BF16)
        qT = sb("qT", [128, 8, TB], BF16)
        ubuf = sb("ubuf", [128, 4, TB + 2], F32)
        NSLOT = 6
        DIRECT_FIRST = True
        ALWAYS_DIRECT = True
        NSTAGE = 1
        wsl = [sb(f"wsl{i}", [128, 2112], BF16) for i in range(NSLOT)]
        stage = Ring([sb(f"stage{i}", [128, D], F32) for i in range(NSTAGE)])
        ostage = Ring([sb(f"ostage{i}", [128, 512], F32) for i in range(2)])
        ftmp = Ring([sb(f"ftmp{i}", [128, TB], F32) for i in range(4)])
        btmp = Ring([sb(f"btmp{i}", [128, TB], BF16) for i in range(6)])
        rstd_ring = Ring([sb(f"rstd{i}", [128, TB], F32) for i in range(1)])
        gnr = Ring([sb(f"gnr{i}", [128, TB], F32) for i in range(1)])
        gsq = Ring([sb(f"gsq{i}", [128, TB], BF16) for i in range(3)])
        gsrc = Ring([sb(f"gsrc{i}", [128, TB], F32) for i in range(3)])
        ident_f = sb("ident_f", [128, 128], F32)
        ones_f = sb("ones_f", [128, 128], F32)
        tri_f = sb("tri_f", [128, 128], F32)
        ident_b = sb("ident_b", [128, 128], BF16)
        mean_b = sb("mean_b", [128, 128], BF16)
        blk_b = sb("blk_b", [128, 128], BF16)
        mask_b = sb("mask_b", [128, 128], BF16)
        vecs = sb("vecs_s", [128, 60], F32)
        ccol = sb("ccol_s", [128, 8], F32)
        scol = sb("scol", [128, 8], F32)
        adabc = sb("adabc_s", [128, 72], F32)
        modc = sb("modc", [128, 72], F32)
        acol = sb("acol", [128, 24], F32)
        gcol = sb("gcol", [128, 24], F32)
        Gall = sb("Gall", [128, 32, 8], F32)
        biasT = sb("biasT", [128, 32, 8], F32)
        carry = sb("carry", [128, 8], F32)
        gref = sb("gref", [128, 8], F32)
        fz = Ring([sb(f"fz{i}", [128, 8], F32) for i in range(4)])

        KTf = KT[:].rearrange("p a s -> p (a s)").bitcast(F32)
        ada_ring = Ring([KTf[:, i * 1024:(i + 1) * 1024] for i in range(8)])
        mod_ops = []
        pbanks = [psb(f"pb{i}", [128, 512], F32) for i in range(8)]
        PP = PsumPool(pbanks)

        R_xT = [Res() for _ in range(8)]
        R_hT = [Res() for _ in range(8)]
        R_yT = [Res() for _ in range(8)]
        R_act = [Res() for _ in range(12)]
        R_qT = [Res() for _ in range(8)]
        R_KT = [[Res() for _ in range(NB)] for _ in range(4)]
        R_V = [Res() for _ in range(32)]
        R_Vall = Res()
        R_u = [Res() for _ in range(4)]
        R_ws = [Res() for _ in range(NSLOT)]
        R_const = Res()
        R_cols = Res()
        R_G = Res()
        R_bias = Res()
        R_carry = Res()
        R_gref = Res()
        R_adaw = [Res() for _ in range(4)]

        ch_ws = [SC.chan() for _ in range(NSLOT)]
        ch_conv = [SC.chan() for _ in range(8)]
        ch_misc = [SC.chan() for _ in range(4)]
        ch_x = [SC.chan() for _ in range(2)]
        ch_out = [SC.chan() for _ in range(2)]
        ch_ada = [SC.chan() for _ in range(4)]

        R_sg = [[Res() for _ in range(11)] for _ in range(2)]
        R_su = [[Res() for _ in range(11)] for _ in range(2)]
        R_sd = [[Res() for _ in range(16)] for _ in range(2)]
        R_sin = [Res() for _ in range(12)]
        R_sout = [Res() for _ in range(4)]
        conv_i = [0]

        DIRECT = {}

        def conv(out_ap, in_ap, res, pat=None):
            if DIRECT_FIRST:
                DIRECT[id(res)] = (in_ap, pat)
                return
            c = ch_conv[conv_i[0] % len(ch_conv)]
            conv_i[0] += 1
            SC.dma("gpsimd", c, out_ap, in_ap, writes=[res])

        def conv_colunit(dst, src, c0, ncol, res, width):
            o = dst[:, 0:8 * ncol].rearrange("p (k f) -> p k f", k=8)
            i = src[:, c0:c0 + ncol].rearrange("(k p) f -> p k f", p=128)
            conv(o, i, res, ("p (k f) -> p k f", dict(k=8)))

        def conv_ffn(l, wg, wu, wd):
            for u in range(11):
                conv_colunit(sg[l][u], wg, u * 256, 256, R_sg[l][u], 2048)
                conv_colunit(su[l][u], wu, u * 256, 256, R_su[l][u], 2048)
            for h, (f0, nf) in enumerate(FH):
                for dc in range(8):
                    o = sd[l][h * 8 + dc][:, 0:nf * 128].rearrange("p (j d) -> p j d", j=nf)
                    i = wd[f0 * 128:(f0 + nf) * 128, dc * 128:(dc + 1) * 128].rearrange(
                        "(j p) d -> p j d", p=128)
                    conv(o, i, R_sd[l][h * 8 + dc], ("p (j d) -> p j d", dict(j=nf)))

        IN_UNITS = [(0, 256), (256, 256), (512, 256), (768, 256), (1024, 256), (1280, 264),
                    (1544, 256), (1800, 256), (2056, 256), (2312, 256), (2568, 256), (2824, 256)]

        def conv_all():
            conv_ffn(0, w1g_d, w1u_d, w1d_d)
            for u, (c0, ncol) in enumerate(IN_UNITS):
                conv_colunit(sin[u], win_d, c0, ncol, R_sin[u], 2112)
            for u in range(4):
                conv_colunit(sout[u], wout_d, u * 256, 256, R_sout[u], 2048)
            conv_ffn(1, w2g_d, w2u_d, w2d_d)

        ws_i = [0]

        def load_unit(src_ap, nelem, src_res):
            k = ws_i[0] % NSLOT
            ws_i[0] += 1
            if ALWAYS_DIRECT and id(src_res) in DIRECT:
                (in_ap, (pstr, pkw)) = DIRECT[id(src_res)]
                SC.dma("gpsimd", ch_ws[k], wsl[k][:, 0:nelem].rearrange(pstr, **pkw), in_ap, writes=[R_ws[k]])
            elif src_res.w is None and id(src_res) in DIRECT:
                (in_ap, (pstr, pkw)) = DIRECT[id(src_res)]
                SC.dma("gpsimd", ch_ws[k], wsl[k][:, 0:nelem].rearrange(pstr, **pkw), in_ap, writes=[R_ws[k]])
                c = ch_conv[conv_i[0] % len(ch_conv)]
                conv_i[0] += 1
                SC.dma("sync", c, src_ap, wsl[k][:, 0:nelem], reads=[R_ws[k]], writes=[src_res])
            else:
                SC.dma("sync", ch_ws[k], wsl[k][:, 0:nelem], src_ap, reads=[src_res], writes=[R_ws[k]])
            return wsl[k], R_ws[k]

        def setup():
            g = "gpsimd"
            SC.op(g, lambda e: e.memset(ident_f[:], 0.0), writes=[R_const])
            SC.op(g, lambda e: e.affine_select(ident_f[:], ident_f[:], [[-1, 128]], ALU.not_equal, 1.0,
                                               base=0, channel_multiplier=1), writes=[R_const])
            SC.op(g, lambda e: e.memset(ones_f[:], 1.0), writes=[R_const])
            SC.op(g, lambda e: e.memset(tri_f[:], 1.0), writes=[R_const])
            SC.op(g, lambda e: e.affine_select(tri_f[:], tri_f[:], [[1, 128]], ALU.is_ge, 0.0,
                                               base=0, channel_multiplier=-1), writes=[R_const])
            SC.op(g, lambda e: e.memset(ident_b[:], 0.0), writes=[R_const])
            SC.op(g, lambda e: e.affine_select(ident_b[:], ident_b[:], [[-1, 128]], ALU.not_equal, 1.0,
                                               base=0, channel_multiplier=1), writes=[R_const])
            SC.op(g, lambda e: e.memset(mean_b[:], 1.0 / 1024.0), writes=[R_const])
            SC.op(g, lambda e: e.memset(blk_b[:], 1.0 / 64.0), writes=[R_const])
            SC.op(g, lambda e: e.memset(blk_b[0:64, 64:128], 0.0), writes=[R_const])
            SC.op(g, lambda e: e.memset(blk_b[64:128, 0:64], 0.0), writes=[R_const])
            SC.op(g, lambda e: e.memset(mask_b[:], 0.0), writes=[R_const])
            SC.op(g, lambda e: e.affine_select(mask_b[:], mask_b[:], [[1, 128]], ALU.is_ge, NEG,
                                               base=0, channel_multiplier=-1), writes=[R_const])
            SC.op(g, lambda e: e.memset(carry[:], 0.0), writes=[R_carry])
            SC.op(g, lambda e: e.memset(ubuf[:], 0.0), writes=R_u)
            SC.op(g, lambda e: e.memset(qT[:], 0.0), writes=R_qT)
            SC.dma("sync", ch_misc[0], vecs[:], vecs_d.ap(), writes=[R_cols])
            SC.dma("sync", ch_misc[1], ccol[:], ccol_d.ap(), writes=[R_cols])

        def mod_setup():
            SC.op("scalar", lambda e: e.activation(scol[:], ccol[:], AF.Silu), reads=[R_cols], writes=[R_cols])
            SC.dma("sync", ch_misc[2], adabc[:], adabc_d.ap(), writes=[R_cols])
            SC.op("gpsimd", lambda e: e.memset(Vc[:, :, :, 64:128], 1.0), writes=[R_Vall] + R_V)

        ada_i = [0]

        def mod_vec(v):
            accs = [ftmp.get(), ftmp.get()]
            for kc in range(8):
                k = ada_i[0] % 4
                ada_i[0] += 1
                (awt, Rawt) = ada_ring.get()
                SC.dma("sync", ch_ada[k], awt,
                       adaw_d[kc * 128:(kc + 1) * 128, v * 1024:(v + 1) * 1024], writes=[Rawt])
                for hc in range(2):
                    (acc, Racc) = accs[hc]
                    if kc == 0:
                        mod_ops.append(SC.op("vector", lambda e, acc=acc, awt=awt, hc=hc, kc=kc: e.tensor_scalar(
                            acc[:], awt[:, hc * 512:(hc + 1) * 512], scol[:, kc:kc + 1], None, ALU.mult),
                            reads=[Rawt, R_cols], writes=[Racc]))
                    else:
                        mod_ops.append(SC.op("vector", lambda e, acc=acc, awt=awt, hc=hc, kc=kc: e.scalar_tensor_tensor(
                            acc[:], awt[:, hc * 512:(hc + 1) * 512], scol[:, kc:kc + 1], acc[:], ALU.mult, ALU.add),
                            reads=[Rawt, R_cols], writes=[Racc]))
            (pc, Rpc) = PP.alloc()
            for fcn in range(8):
                (acc, Racc) = accs[fcn // 4]
                SC.op("tensor", lambda e, pc=pc, acc=acc, fcn=fcn: e.matmul(
                    pc[:, fcn:fcn + 1], acc[:, (fcn % 4) * 128:(fcn % 4 + 1) * 128], ones_f[:, 0:1],
                    start=True, stop=True), reads=[Racc, R_const], writes=[Rpc])
            SC.op("vector", lambda e, pc=pc, v=v: e.tensor_tensor(
                modc[:, v * 8:(v + 1) * 8], pc[:, 0:8], adabc[:, v * 8:(v + 1) * 8], ALU.add),
                reads=[Rpc, R_cols], writes=[R_cols])
            PP.release((pc, Rpc))

        def mod_derive(l):
            sc_ = modc[:, (3 * l + 1) * 8:(3 * l + 2) * 8]
            ng = vecs[:, l * 8:(l + 1) * 8]
            SC.op("vector", lambda e, sc_=sc_, ng=ng, l=l: e.scalar_tensor_tensor(
                acol[:, l * 8:(l + 1) * 8], sc_, 1.0, ng, ALU.add, ALU.mult),
                reads=[R_cols], writes=[R_cols])

        def mod_gate(l):
            g_ = modc[:, (3 * l + 2) * 8:(3 * l + 3) * 8]
            fac = 1.0 if l == 1 else 0.5
            SC.op("vector", lambda e, g_=g_, l=l, fac=fac: e.tensor_scalar(
                gcol[:, l * 8:(l + 1) * 8], g_, fac, None, ALU.mult),
                reads=[R_cols], writes=[R_cols])

        def shcol(l, dc):
            return modc[:, 3 * l * 8 + dc:3 * l * 8 + dc + 1]

        xq = []
        actTf = actT[:].rearrange("p j t -> p (j t)").bitcast(F32)
        xtiles = [(stage.tiles[0][0][:], [stage.tiles[0][1]])]
        for k_ in range(3):
            xtiles.append((actTf[:, k_ * 1024:(k_ + 1) * 1024], R_act[4 * k_:4 * k_ + 4]))

        def prefetch_x(i, tts):
            for tt in tts:
                (stg, Rsl) = xtiles[tt]
                r0 = (4 * i + tt) * 128
                SC.dma("sync", ch_x[tt % 2], stg, x_d[r0:r0 + 128, :], writes=Rsl)
                xq.append((stg, Rsl))

        pre_tr = {}

        def load_transposes(stg, Rsl):
            banks = []
            for half in range(2):
                (pt, Rpt) = PP.alloc()
                for q in range(4):
                    dc = half * 4 + q
                    SC.op("tensor", lambda e, pt=pt, q=q, stg=stg, dc=dc: e.transpose(
                        pt[:, q * 128:(q + 1) * 128], stg[:, dc * 128:(dc + 1) * 128], ident_f[:]),
                        reads=list(Rsl) + [R_const], writes=[Rpt])
                banks.append((pt, Rpt))
            return banks

        def load_pre(i):
            (stg, Rsl) = xq.pop(0)
            pre_tr[i] = load_transposes(stg, Rsl)

        def load_block(i):
            for tt in range(4):
                if tt == 0 and i in pre_tr:
                    banks = pre_tr.pop(i)
                else:
                    (stg, Rsl) = xq.pop(0)
                    banks = load_transposes(stg, Rsl)
                for half, (pt, Rpt) in enumerate(banks):
                    eng = "scalar" if half == 0 else "vector"
                    o = xT[:, half * 4:half * 4 + 4, tt * 128:(tt + 1) * 128]
                    src = pt[:, :].rearrange("p (q t) -> p q t", q=4)
                    if eng == "scalar":
                        SC.op(eng, lambda e, o=o, src=src: e.copy(o, src), reads=[Rpt],
                              writes=R_xT[half * 4:half * 4 + 4])
                    else:
                        SC.op(eng, lambda e, o=o, src=src: e.tensor_copy(o, src), reads=[Rpt],
                              writes=R_xT[half * 4:half * 4 + 4])
                    PP.release((pt, Rpt))

        def norm_stats():
            (ps, Rps) = PP.alloc()
            for dc in range(8):
                (sq, Rsq) = btmp.get()
                SC.op("scalar", lambda e, sq=sq, dc=dc: e.activation(sq[:], xT[:, dc, :], AF.Square),
                      reads=[R_xT[dc]], writes=[Rsq])
                SC.op("tensor", lambda e, ps=ps, sq=sq, dc=dc: e.matmul(
                    ps[:], mean_b[:], sq[:], start=(dc == 0), stop=(dc == 7)),
                    reads=[Rsq, R_const], writes=[Rps])
            (rs, Rrs) = rstd_ring.get()
            SC.op("scalar", lambda e, rs=rs, ps=ps: e.activation(rs[:], ps[:], AF.Ln, bias=EPS, scale=1.0),
                  reads=[Rps], writes=[Rrs])
            PP.release((ps, Rps))
            SC.op("scalar", lambda e, rs=rs: e.activation(rs[:], rs[:], AF.Exp, scale=-0.5),
                  reads=[], writes=[Rrs])
            return rs, Rrs

        def norm_mod(l):
            rs, Rrs = norm_stats()
            for dc in range(8):
                (t, Rt) = ftmp.get()
                SC.op("vector", lambda e, t=t, dc=dc, rs=rs, l=l: e.scalar_tensor_tensor(
                    t[:], xT[:, dc, :], acol[:, l * 8 + dc:l * 8 + dc + 1], rs[:], ALU.mult, ALU.mult),
                    reads=[R_xT[dc], Rrs, R_cols], writes=[Rt])
                SC.op("scalar", lambda e, t=t, dc=dc, l=l: e.activation(
                    hT[:, dc, :], t[:], AF.Identity, bias=shcol(l, dc), scale=1.0),
                    reads=[Rt, R_cols], writes=[R_hT[dc]])

        def ffn(l, gl, hook=None):
            for h, (f0, nf) in enumerate(FH):
                units = {}

                def unit(u):
                    if u not in units:
                        gs, Rgs = load_unit(sg[l][u], 2048, R_sg[l][u])
                        us, Rus = load_unit(su[l][u], 2048, R_su[l][u])
                        units[u] = (gs[:, 0:2048].rearrange("p (k f) -> p k f", k=8), Rgs,
                                    us[:, 0:2048].rearrange("p (k f) -> p k f", k=8), Rus)
                    return units[u]

                groups = [[f0, f0 + 1, f0 + 2]] + [[fc] for fc in range(f0 + 3, f0 + nf)] if h == 0 \
                    else [[fc] for fc in range(f0, f0 + nf)]
                for grp in groups:
                    info = []
                    for fc in grp:
                        gv, Rgs, uv, Rus = unit(fc // 2)
                        info.append((fc, gv, Rgs, uv, Rus, PP.alloc(), PP.alloc()))
                    if len(grp) == 1:
                        order = [(it, w, kc) for it in info for w in (0, 1) for kc in range(8)]
                    else:
                        order = [(it, w, kc) for kc in range(8) for it in info for w in (0, 1)]
                    for (it, w, kc) in order:
                        (fc, gv, Rgs, uv, Rus, (pg, Rpg), (pu, Rpu)) = it
                        s0 = (fc % 2) * 128
                        if w == 0:
                            SC.op("tensor", lambda e, pg=pg, gv=gv, kc=kc, s0=s0: e.matmul(
                                pg[:], gv[:, kc, s0:s0 + 128], hT[:, kc, :], start=(kc == 0), stop=(kc == 7)),
                                reads=[Rgs, R_hT[kc]], writes=[Rpg])
                        else:
                            SC.op("tensor", lambda e, pu=pu, uv=uv, kc=kc, s0=s0: e.matmul(
                                pu[:], uv[:, kc, s0:s0 + 128], hT[:, kc, :], start=(kc == 0), stop=(kc == 7)),
                                reads=[Rus, R_hT[kc]], writes=[Rpu])
                    for it in info:
                        (fc, gv, Rgs, uv, Rus, (pg, Rpg), (pu, Rpu)) = it
                        j = fc - f0
                        (sl, Rsl) = btmp.get()
                        SC.op("scalar", lambda e, sl=sl, pg=pg: e.activation(sl[:], pg[:], AF.Silu),
                              reads=[Rpg], writes=[Rsl])
                        PP.release((pg, Rpg))
                        SC.op("vector", lambda e, j=j, pu=pu, sl=sl: e.tensor_tensor(
                            actT[:, j, :], pu[:], sl[:], ALU.mult), reads=[Rpu, Rsl], writes=[R_act[j]])
                        PP.release((pu, Rpu))
                if hook is not None:
                    hook(h)
                for dc in range(8):
                    ds, Rds = load_unit(sd[l][h * 8 + dc][:, 0:nf * 128], nf * 128, R_sd[l][h * 8 + dc])
                    dv = ds[:, 0:nf * 128].rearrange("p (j d) -> p j d", j=nf)
                    (py, Rpy) = PP.alloc()
                    for j in range(nf):
                        SC.op("tensor", lambda e, py=py, dv=dv, j=j, nf=nf: e.matmul(
                            py[:], dv[:, j, :], actT[:, j, :], start=(j == 0), stop=(j == nf - 1)),
                            reads=[Rds, R_act[j]], writes=[Rpy])
                    SC.op("vector", lambda e, py=py, dc=dc, gl=gl: e.scalar_tensor_tensor(
                        xT[:, dc, :], py[:], gcol[:, gl * 8 + dc:gl * 8 + dc + 1], xT[:, dc, :],
                        ALU.mult, ALU.add), reads=[Rpy, R_cols], writes=[R_xT[dc]])
                    PP.release((py, Rpy))

        gn_pending = []

        def groupnorm_to_y(src, Rsrc, kc):
            gn_pending.append((src, Rsrc, kc))

        def gn_flush(keep=0):
            while len(gn_pending) > keep:
                (src, Rsrc, kc) = gn_pending.pop(0)
                (sq, Rsq) = gsq.get()
                SC.op("scalar", lambda e, sq=sq, src=src: e.activation(sq[:], src[:], AF.Square),
                      reads=[Rsrc], writes=[Rsq])
                (ps, Rps) = PP.alloc()
                SC.op("tensor", lambda e, ps=ps, sq=sq: e.matmul(ps[:], blk_b[:], sq[:], start=True, stop=True),
                      reads=[Rsq, R_const], writes=[Rps])
                (rs, Rrs) = gnr.get()
                SC.op("scalar", lambda e, rs=rs, ps=ps: e.activation(rs[:], ps[:], AF.Ln, bias=EPS, scale=1.0),
                      reads=[Rps], writes=[Rrs])
                PP.release((ps, Rps))
                SC.op("scalar", lambda e, rs=rs: e.activation(rs[:], rs[:], AF.Exp, scale=-0.5),
                      reads=[], writes=[Rrs])
                SC.op("vector", lambda e, src=src, rs=rs, kc=kc: e.scalar_tensor_tensor(
                    yT[:, kc, :], src[:], vecs[:, 32 + kc:33 + kc], rs[:], ALU.mult, ALU.mult),
                    reads=[Rsrc, Rrs, R_cols], writes=[R_yT[kc]])

        def in_proj(i):
            SC.op("vector", lambda e: e.tensor_copy(gref[:], carry[:]), reads=[R_carry], writes=[R_gref])
            for up in range(2):
                cs, Rc = load_unit(sin[8 + up][:, 0:2048], 2048, R_sin[8 + up])
                xs, Rx = load_unit(sin[10 + up][:, 0:2048], 2048, R_sin[10 + up])
                bs, Rb = load_unit(sin[6 + up][:, 0:2048], 2048, R_sin[6 + up])
                bv = bs[:, 0:2048].rearrange("p (k f) -> p k f", k=8)
                cv_ = cs[:, 0:2048].rearrange("p (k f) -> p k f", k=8)
                xv = xs[:, 0:2048].rearrange("p (k f) -> p k f", k=8)
                for s_ in range(2):
                    cc = up * 2 + s_
                    outs = [PP.alloc() for _ in range(3)]
                    for kc in range(8):
                        for (wv, Rw), (pp, Rpp) in zip(((cv_, Rc), (xv, Rx), (bv, Rb)), outs):
                            SC.op("tensor", lambda e, pp=pp, wv=wv, kc=kc, s_=s_: e.matmul(
                                pp[:], wv[:, kc, s_ * 128:(s_ + 1) * 128], hT[:, kc, :],
                                start=(kc == 0), stop=(kc == 7)), reads=[Rw, R_hT[kc]], writes=[Rpp])
                    (pC, RpC), (pX, RpX), (pB, RpB) = outs
                    gn_flush(keep=1)
                    (csb, Rcsb) = ftmp.get()
                    SC.op("scalar", lambda e, csb=csb, pC=pC: e.copy(csb[:], pC[:]), reads=[RpC], writes=[Rcsb])
                    PP.release((pC, RpC))
                    SC.op("vector", lambda e, cc=cc, pX=pX, csb=csb: e.tensor_tensor(
                        ubuf[:, cc, 2:TB + 2], pX[:], csb[:], ALU.mult), reads=[RpX, Rcsb], writes=[R_u[cc]])
                    PP.release((pX, RpX))
                    (bsb, Rbsb) = gsrc.get()
                    SC.op("scalar", lambda e, bsb=bsb, pB=pB: e.copy(bsb[:], pB[:]), reads=[RpB], writes=[Rbsb])
                    PP.release((pB, RpB))
                    w0 = vecs[:, 40 + cc:41 + cc]
                    w1 = vecs[:, 44 + cc:45 + cc]
                    w2 = vecs[:, 48 + cc:49 + cc]
                    (y1, Ry1) = ftmp.get()
                    SC.op("scalar", lambda e, y1=y1, cc=cc, w2=w2: e.activation(
                        y1[:], ubuf[:, cc, 2:TB + 2], AF.Identity, scale=w2), reads=[R_u[cc], R_cols], writes=[Ry1])
                    SC.op("vector", lambda e, y1=y1, cc=cc, w1=w1: e.scalar_tensor_tensor(
                        y1[:], ubuf[:, cc, 1:TB + 1], w1, y1[:], ALU.mult, ALU.add),
                        reads=[R_u[cc], R_cols], writes=[Ry1])
                    SC.op("vector", lambda e, y1=y1, cc=cc, w0=w0: e.scalar_tensor_tensor(
                        y1[:], ubuf[:, cc, 0:TB], w0, y1[:], ALU.mult, ALU.add),
                        reads=[R_u[cc], R_cols], writes=[Ry1])
                    SC.op("vector", lambda e, cc=cc: e.tensor_copy(ubuf[:, cc, 0:2], ubuf[:, cc, TB:TB + 2]),
                          reads=[], writes=[R_u[cc]])
                    SC.op("vector", lambda e, bsb=bsb, y1=y1: e.tensor_tensor(
                        bsb[:], bsb[:], y1[:], ALU.mult), reads=[Ry1], writes=[Rbsb])
                    groupnorm_to_y(bsb, Rbsb, 4 + cc)
            for u in range(4):
                ws, Rw = load_unit(sin[u][:, 0:2048], 2048, R_sin[u])
                wv = ws[:, 0:2048].rearrange("p (k f) -> p k f", k=8)
                for s_ in range(2):
                    j = (u % 2) * 2 + s_
                    (pq, Rpq) = PP.alloc()
                    for kc in range(8):
                        SC.op("tensor", lambda e, pq=pq, wv=wv, kc=kc, s_=s_: e.matmul(
                            pq[:], wv[:, kc, s_ * 128:(s_ + 1) * 128], hT[:, kc, :],
                            start=(kc == 0), stop=(kc == 7)), reads=[Rw, R_hT[kc]], writes=[Rpq])
                    if u < 2:
                        SC.op("scalar", lambda e, pq=pq, j=j: e.copy(qT[0:64, 2 * j, :], pq[0:64, :]),
                              reads=[Rpq], writes=[R_qT[2 * j]])
                        SC.op("scalar", lambda e, pq=pq, j=j: e.copy(qT[64:128, 2 * j + 1, :], pq[64:128, :]),
                              reads=[Rpq], writes=[R_qT[2 * j + 1]])
                    else:
                        SC.op("vector", lambda e, pq=pq, j=j, i=i: e.tensor_copy(
                            KT[:, j, i * TB:(i + 1) * TB], pq[:]), reads=[Rpq], writes=[R_KT[j][i]],
                            extra=(mod_ops[-1:] if i == 0 else ()))
                    PP.release((pq, Rpq))
            gn_flush()
            v0, Rv0 = load_unit(sin[4][:, 0:2048], 2048, R_sin[4])
            v1, Rv1 = load_unit(sin[5][:, 0:2112], 2112, R_sin[5])
            v0v = v0[:, 0:2048].rearrange("p (k f) -> p k f", k=8)
            v1v = v1[:, 0:2112].rearrange("p (k f) -> p k f", k=8)
            for tt in range(4):
                kb = 4 * i + tt
                (pv0, Rpv0) = PP.alloc()
                (pv1, Rpv1) = PP.alloc()
                for kc in range(8):
                    SC.op("tensor", lambda e, pv0=pv0, kc=kc, tt=tt: e.matmul(
                        pv0[:, 0:256], hT[:, kc, tt * 128:(tt + 1) * 128], v0v[:, kc, :],
                        start=(kc == 0), stop=(kc == 7)), reads=[Rv0, R_hT[kc]], writes=[Rpv0])
                for kc in range(8):
                    SC.op("tensor", lambda e, pv1=pv1, kc=kc, tt=tt: e.matmul(
                        pv1[:, 0:264], hT[:, kc, tt * 128:(tt + 1) * 128], v1v[:, kc, :],
                        start=(kc == 0), stop=(kc == 7)), reads=[Rv1, R_hT[kc]], writes=[Rpv1])
                (z, Rz) = fz.get()
                SC.op("vector", lambda e, z=z, pv1=pv1: e.tensor_tensor(
                    z[:], pv1[:, 256:264], vecs[:, 52:60], ALU.add), reads=[Rpv1, R_cols], writes=[Rz])
                for g_, (pvx, Rpvx) in enumerate(((pv0, Rpv0), (pv1, Rpv1))):
                    src = pvx[:, 0:256].rearrange("p (a b d) -> p a b d", a=2, b=2)
                    dst = Vc[:, kb, 2 * g_:2 * g_ + 2, :].rearrange("p a (b d) -> p a b d", b=3)[:, :, 0:3:2, :]
                    if g_ == 0:
                        SC.op("scalar", lambda e, dst=dst, src=src: e.copy(dst, src),
                              reads=[Rpvx], writes=[R_V[kb]])
                    else:
                        SC.op("vector", lambda e, dst=dst, src=src: e.tensor_copy(dst, src),
                              reads=[Rpvx], writes=[R_V[kb]])
                PP.release((pv0, Rpv0))
                PP.release((pv1, Rpv1))
                (ez, Rez) = fz.get()
                SC.op("scalar", lambda e, ez=ez, z=z: e.activation(ez[:], z[:], AF.Exp, scale=-1.0),
                      reads=[Rz], writes=[Rez])
                (sp, Rsp) = fz.get()
                SC.op("scalar", lambda e, sp=sp, ez=ez: e.activation(sp[:], ez[:], AF.Ln, bias=1.0, scale=1.0),
                      reads=[Rez], writes=[Rsp])
                (pf, Rpf) = PP.alloc()
                SC.op("tensor", lambda e, pf=pf, sp=sp: e.matmul(pf[:, 0:8], tri_f[:], sp[:], start=True, stop=True),
                      reads=[Rsp, R_const], writes=[Rpf])
                SC.op("tensor", lambda e, pf=pf, sp=sp: e.matmul(pf[:, 8:16], ones_f[:], sp[:], start=True, stop=True),
                      reads=[Rsp, R_const], writes=[Rpf])
                SC.op("vector", lambda e, pf=pf, kb=kb: e.tensor_tensor(
                    Gall[:, kb, :], pf[:, 0:8], carry[:], ALU.add), reads=[Rpf, R_carry], writes=[R_G])
                SC.op("vector", lambda e, pf=pf: e.tensor_tensor(
                    carry[:], pf[:, 8:16], carry[:], ALU.add), reads=[Rpf], writes=[R_carry])
                PP.release((pf, Rpf))

        LA = 4
        FASTRECIP = False

        def attention(i):
            nkb = 4 * i + 4
            SC.op("vector", lambda e: e.tensor_tensor(gref[:], gref[:], carry[:], ALU.add),
                  reads=[R_carry], writes=[R_gref])
            SC.op("vector", lambda e: e.tensor_scalar(gref[:], gref[:], 0.5, None, ALU.mult),
                  reads=[], writes=[R_gref])
            for h in range(8):
                SC.op("vector", lambda e, h=h, nkb=nkb: e.tensor_scalar(
                    biasT[:, 0:nkb, h], Gall[:, 0:nkb, h], gref[:, h:h + 1], None, ALU.subtract),
                    reads=[R_G, R_gref], writes=[R_bias])
            steps = [(h, kb) for h in range(8) for kb in range(nkb)]
            sbank = {}
            fin_pending = []

            def emit_qk(si):
                h, kb = steps[si]
                pr, half = h // 2, h % 2
                b0 = 64 * half
                dq = kb - 4 * i
                n0 = max(0, dq) * 128
                (ps, Rps) = PP.alloc()
                sbank[si] = (ps, Rps)
                SC.op("tensor", lambda e, ps=ps, pr=pr, kb=kb, h=h, n0=n0, dq=dq: e.matmul(
                    ps[:, n0:TB], KT[:, pr, kb * 128:(kb + 1) * 128], qT[:, h, n0:TB],
                    start=True, stop=(dq < 0)),
                    reads=[R_KT[pr][kb // 4], R_qT[h]], writes=[Rps])
                if dq >= 0:
                    SC.op("tensor", lambda e, ps=ps, n0=n0: e.matmul(
                        ps[:, n0:n0 + 128], ident_b[:], mask_b[:], start=False, stop=True),
                        reads=[R_const], writes=[Rps])

            for si in range(min(LA, len(steps))):
                emit_qk(si)
            cur = {}
            for si, (h, kb) in enumerate(steps):
                pr, half = h // 2, h % 2
                b0 = 64 * half
                dq = kb - 4 * i
                n0 = max(0, dq) * 128
                if si + LA < len(steps):
                    emit_qk(si + LA)
                if kb == 0:
                    cur["po"] = PP.alloc()
                    if half == 0:
                        cur["onp"] = gsrc.get()
                if kb == min(10, nkb - 1) and half == 0:
                    gn_flush()
                (po, Rpo) = cur["po"]
                (onp, Ronp) = cur["onp"]
                (ps, Rps) = sbank.pop(si)
                (pt, Rpt) = btmp.get()
                SC.op("scalar", lambda e, pt=pt, ps=ps, n0=n0, kb=kb, h=h: e.activation(
                    pt[:, n0:TB], ps[:, n0:TB], AF.Exp, bias=biasT[:, kb, h:h + 1], scale=0.125),
                    reads=[Rps, R_bias], writes=[Rpt])
                PP.release((ps, Rps))
                off = 64 * half
                SC.op("tensor", lambda e, po=po, pt=pt, kb=kb, pr=pr, off=off, n0=n0, nkb=nkb: e.matmul(
                    po[:, n0:TB], Vc[:, kb, pr, off:off + 128], pt[:, n0:TB],
                    start=(kb == 0), stop=(kb == nkb - 1)),
                    reads=[R_V[kb], Rpt], writes=[Rpo])
                while fin_pending and fin_pending[0][0] <= si:
                    fin_pending.pop(0)[1]()
                if kb == nkb - 1:
                    def finalize(po=po, Rpo=Rpo, onp=onp, Ronp=Ronp, b0=b0, half=half, pr=pr):
                        d0 = 64 - b0
                        (osb, Rosb) = ftmp.get()
                        SC.op("scalar", lambda e: e.copy(osb[:], po[:]), reads=[Rpo], writes=[Rosb])
                        PP.release((po, Rpo))
                        (rc, Rrc) = ftmp.get()
                        if FASTRECIP:
                            SC.op("vector", lambda e: e.reciprocal_approx_fast(
                                rc[b0:b0 + 64, :], osb[d0:d0 + 64, :]), reads=[Rosb], writes=[Rrc])
                        else:
                            SC.op("vector", lambda e: e.reciprocal(
                                rc[b0:b0 + 64, :], osb[d0:d0 + 64, :]), reads=[Rosb], writes=[Rrc])
                        SC.op("vector", lambda e: e.tensor_tensor(
                            onp[b0:b0 + 64, :], osb[b0:b0 + 64, :], rc[b0:b0 + 64, :], ALU.mult),
                            reads=[Rosb, Rrc], writes=[Ronp])
                        if half == 1:
                            groupnorm_to_y(onp, Ronp, pr)
                    fin_pending.append((si + 2, finalize))
            while fin_pending:
                fin_pending.pop(0)[1]()

        def out_proj():
            units = {}

            def unit(u):
                if u not in units:
                    ws, Rw = load_unit(sout[u], 2048, R_sout[u])
                    units[u] = (ws[:, 0:2048].rearrange("p (k f) -> p k f", k=8), Rw)
                return units[u]

            def mm(py, Rpy, dc, kcs, first, last):
                wv, Rw = unit(dc // 2)
                s_ = dc % 2
                for n_, kc in enumerate(kcs):
                    SC.op("tensor", lambda e, py=py, wv=wv, kc=kc, s_=s_, n_=n_: e.matmul(
                        py[:], wv[:, kc, s_ * 128:(s_ + 1) * 128], yT[:, kc, :],
                        start=(first and n_ == 0), stop=(last and n_ == len(kcs) - 1)),
                        reads=[Rw, R_yT[kc]], writes=[Rpy])

            def evac(py, Rpy, dc):
                SC.op("vector", lambda e, py=py, dc=dc: e.scalar_tensor_tensor(
                    xT[:, dc, :], py[:], gcol[:, 8 + dc:9 + dc], xT[:, dc, :], ALU.mult, ALU.add),
                    reads=[Rpy, R_cols], writes=[R_xT[dc]])
                PP.release((py, Rpy))

            held = []
            for dc in range(6):
                (py, Rpy) = PP.alloc()
                mm(py, Rpy, dc, (4, 5, 6, 7), True, False)
                held.append((py, Rpy, dc))
            gn_flush()
            for (py, Rpy, dc) in held:
                mm(py, Rpy, dc, (0, 1, 2, 3), False, True)
                evac(py, Rpy, dc)
            for dc in (6, 7):
                (py, Rpy) = PP.alloc()
                mm(py, Rpy, dc, (4, 5, 6, 7, 0, 1, 2, 3), True, True)
                evac(py, Rpy, dc)

        store_ops = []

        def final_store(i):
            if fnorm:
                rs, Rrs = norm_stats()
                for dc in range(8):
                    SC.op("vector", lambda e, dc=dc, rs=rs: e.scalar_tensor_tensor(
                        xT[:, dc, :], xT[:, dc, :], vecs[:, 24 + dc:25 + dc], rs[:], ALU.mult, ALU.mult),
                        reads=[Rrs, R_cols], writes=[R_xT[dc]])
            for tt in range(4):
                r0 = (4 * i + tt) * 128
                for half in range(2):
                    (stg, Rs) = ostage.get()
                    (pt, Rpt) = PP.alloc()
                    for q in range(4):
                        dc = half * 4 + q
                        SC.op("tensor", lambda e, pt=pt, q=q, dc=dc, tt=tt: e.transpose(
                            pt[:, q * 128:(q + 1) * 128], xT[:, dc, tt * 128:(tt + 1) * 128], ident_f[:]),
                            reads=[R_xT[dc], R_const], writes=[Rpt])
                    if half == 0:
                        SC.op("scalar", lambda e, stg=stg, pt=pt: e.copy(stg[:], pt[:]),
                              reads=[Rpt], writes=[Rs])
                    else:
                        SC.op("vector", lambda e, stg=stg, pt=pt: e.tensor_copy(stg[:], pt[:]),
                              reads=[Rpt], writes=[Rs])
                    PP.release((pt, Rpt))
                    o = SC.dma("gpsimd", ch_out[half], out_d[r0:r0 + 128, half * 512:(half + 1) * 512], stg[:],
                               reads=[Rs])
                    store_ops.append(o)

        setup()
        mod_setup()
        mod_vec(0)
        mod_vec(1)
        mod_derive(0)
        conv_all()
        prefetch_x(0, [0, 1, 2, 3])
        for i in range(nblocks):
            load_block(i)
            if upto >= 1:
                norm_mod(0)
                if dbg and i == 0:
                    chd = SC.chan()
                    store_ops.append(SC.dma("gpsimd", chd, dbg_d[:, 0:72], modc[:], reads=[R_cols]))
                    store_ops.append(SC.dma("gpsimd", chd, dbg_d[:, 72:96], acol[:], reads=[R_cols]))
                    store_ops.append(SC.dma("gpsimd", chd, dbg_d[:, 96:120], gcol[:], reads=[R_cols]))
                    store_ops.append(SC.dma("gpsimd", chd, dbgh_d.ap().rearrange("p (k t) -> p k t", k=8), hT[:], reads=R_hT))
                if i == 0:
                    def hook(h):
                        if h == 0:
                            mod_vec(2)
                            mod_gate(0)
                            mod_vec(3)
                            mod_vec(4)
                            mod_derive(1)
                        else:
                            mod_vec(5)
                            mod_gate(1)
                            mod_vec(6)
                            mod_vec(7)
                            mod_derive(2)
                            mod_vec(8)
                            mod_gate(2)
                    ffn(0, 0, hook)
                else:
                    ffn(0, 0)
            if upto >= 2:
                norm_mod(1)
                in_proj(i)
                attention(i)
                out_proj()
            if i + 1 < nblocks:
                prefetch_x(i + 1, [0])
            if upto >= 3:
                norm_mod(2)
                ffn(1, 2)
            if i + 1 < nblocks:
                prefetch_x(i + 1, [1, 2, 3])
                load_pre(i + 1)
            final_store(i)
        SC.op("gpsimd", None, extra=store_ops[-12:])
        SC.emit()
    return nc


_CACHE = {}


def _col(v):
    return np.ascontiguousarray(np.asarray(v, np.float32).reshape(-1, 128).T)


def kernel(x, c, ada_w, ada_b, norm1_g, ffn1_w_gate, ffn1_w_up, ffn1_w_down,
           norm2_g, w_in, forget_bias, conv_w, group_norm_g, w_out,
           norm3_g, ffn2_w_gate, ffn2_w_up, ffn2_w_down, final_g):
    f32 = lambda a: np.ascontiguousarray(np.asarray(a, dtype=np.float32))
    x = f32(x)
    c = f32(c)
    vecs = np.zeros((128, 60), np.float32)
    vecs[:, 0:8] = _col(norm1_g)
    vecs[:, 8:16] = _col(norm2_g)
    vecs[:, 16:24] = _col(norm3_g)
    vecs[:, 24:32] = _col(final_g)
    vecs[:, 32:40] = _col(group_norm_g)
    cw = f32(conv_w)
    for j in range(3):
        vecs[:, 40 + 4 * j:44 + 4 * j] = _col(cw[j])
    vecs[:, 52:60] = np.broadcast_to(f32(forget_bias)[None, :], (128, 8))
    shared = {
        "ada_w": f32(ada_w), "adabc": _col(ada_b), "vecs": vecs,
        "w1g": f32(ffn1_w_gate), "w1u": f32(ffn1_w_up), "w1d": f32(ffn1_w_down),
        "w_in": f32(w_in), "w_out": f32(w_out),
        "w2g": f32(ffn2_w_gate), "w2u": f32(ffn2_w_up), "w2d": f32(ffn2_w_down),
    }
    in_maps = []
    for b in range(8):
        m = dict(shared)
        m["x"] = x[b]
        m["ccol"] = _col(c[b])
        in_maps.append(m)
    if "nc" not in _CACHE:
        _CACHE["nc"] = build_program()
    nc = _CACHE["nc"]
    res = run_bass_kernel_spmd(nc, in_maps, core_ids=list(range(8)))
    return np.stack([np.asarray(r["out"], dtype=np.float32) for r in res.results], axis=0)
```
